# Optimizing a Trainium2 kernel written in Bass

```python
import math
import jax, jax.numpy as jnp
from jax import lax
import numpy as np

D_MODEL = 1024
BATCH = 4
SEQ = 8192
DEPTH = 1

D_MIX = D_MODEL
HEAD_DIM = 64
N_Q_HEADS = 8
N_KV_HEADS = 2
Q_PER_KV = N_Q_HEADS // N_KV_HEADS
D_ATTN = N_Q_HEADS * HEAD_DIM
D_CONV = D_MIX - D_ATTN
CONV_WIDTH = 31
WINDOW = 128
BLOCK = 128
N_BUCKETS = 32
MAX_EXACT = N_BUCKETS // 2
MAX_DISTANCE = 128
D_FF = int(math.ceil(8 * D_MODEL / 3 / 256) * 256)
EPS = 1e-6
D_Q = N_Q_HEADS * HEAD_DIM
D_KV = N_KV_HEADS * HEAD_DIM
D_IN = D_Q + 2 * D_KV + 2 * D_CONV

kernel_name = "hymba_conformer_swa_sink_hybrid"


def rms_norm(x, g):
    xf = x.astype(jnp.float32)
    y = xf * lax.rsqrt(jnp.mean(xf * xf, axis=-1, keepdims=True) + EPS)
    return (y * g.astype(jnp.float32)).astype(x.dtype)


def layer_norm(x, g, b):
    xf = x.astype(jnp.float32)
    mu = jnp.mean(xf, axis=-1, keepdims=True)
    var = jnp.mean(jnp.square(xf - mu), axis=-1, keepdims=True)
    y = (xf - mu) * lax.rsqrt(var + EPS)
    return (y * g.astype(jnp.float32) + b.astype(jnp.float32)).astype(x.dtype)


def t5_causal_bucket(dist):
    is_small = dist < MAX_EXACT
    d = jnp.maximum(dist, 1).astype(jnp.float32)
    large = MAX_EXACT + (jnp.log(d / MAX_EXACT) / math.log(MAX_DISTANCE / MAX_EXACT)
                         * (N_BUCKETS - MAX_EXACT)).astype(jnp.int32)
    large = jnp.minimum(large, N_BUCKETS - 1)
    return jnp.where(is_small, dist, large)


def conformer_conv(u, conv_dw, conv_dw_b, conv_ln_g, conv_ln_b, w_conv_pw):
    a, gate = jnp.split(u, 2, axis=-1)
    h = a * jax.nn.sigmoid(gate)
    h = lax.conv_general_dilated(
        h, conv_dw[:, None, :].astype(h.dtype), window_strides=(1,),
        padding=[(CONV_WIDTH - 1, 0)],
        dimension_numbers=("NWC", "WIO", "NWC"),
        feature_group_count=D_CONV) + conv_dw_b
    h = jax.nn.silu(layer_norm(h, conv_ln_g, conv_ln_b))
    return h @ w_conv_pw


def sliding_window_attention(q, k, v, attn_sinks, rel_bias):
    B, S = q.shape[0], q.shape[1]
    nb = S // BLOCK
    qb = q.reshape(B, nb, BLOCK, N_KV_HEADS, Q_PER_KV, HEAD_DIM)
    pad = ((0, 0), (BLOCK, 0), (0, 0), (0, 0))
    kp, vp = jnp.pad(k, pad), jnp.pad(v, pad)

    def band(t):
        prev = t[:, :S].reshape(B, nb, BLOCK, N_KV_HEADS, HEAD_DIM)
        cur = t[:, BLOCK:].reshape(B, nb, BLOCK, N_KV_HEADS, HEAD_DIM)
        return jnp.concatenate([prev, cur], axis=2)

    kb, vb = band(kp), band(vp)
    scale = 1.0 / math.sqrt(HEAD_DIM)
    s = jnp.einsum("bnqhgd,bnkhd->bnhgqk", qb, kb).astype(jnp.float32) * scale

    qi = jnp.arange(BLOCK)[:, None]
    kj = jnp.arange(2 * BLOCK)[None, :]
    dist = qi + BLOCK - kj
    in_band = (dist >= 0) & (dist < WINDOW)
    bucket = t5_causal_bucket(jnp.maximum(dist, 0))
    bias = rel_bias.astype(jnp.float32)[bucket]
    bias = jnp.transpose(bias, (2, 0, 1)).reshape(N_KV_HEADS, Q_PER_KV, BLOCK, 2 * BLOCK)
    blk = jnp.arange(nb)[:, None, None]
    valid = in_band[None] & ((blk > 0) | (kj[None] >= BLOCK))
    s = jnp.where(valid[None, :, None, None], s + bias, -jnp.inf)

    sink = attn_sinks.astype(jnp.float32).reshape(N_KV_HEADS, Q_PER_KV, 1, 1)
    m = jnp.maximum(jnp.max(s, axis=-1, keepdims=True), sink)
    p = jnp.exp(s - m)
    denom = jnp.sum(p, axis=-1, keepdims=True) + jnp.exp(sink - m)
    p = (p / denom).astype(v.dtype)
    o = jnp.einsum("bnhgqk,bnkhd->bnqhgd", p, vb)
    return o.reshape(B, S, N_Q_HEADS * HEAD_DIM)


def setup_inputs(seed: int = 0) -> dict:
    key = jax.random.key(seed)
    ks = jax.random.split(key, 20)
    f = jnp.float32
    nrm = lambda k, shp, sc: jax.random.normal(k, shp, f) * sc
    gain = lambda k, n: 1.0 + 0.05 * jax.random.normal(k, (n,), f)
    return {
        "x": jax.random.normal(ks[0], (BATCH, SEQ, D_MODEL), f),
        "mix_pre_g": gain(ks[1], D_MODEL),
        "mix_post_g": gain(ks[2], D_MODEL),
        "ffn_pre_g": gain(ks[3], D_MODEL),
        "ffn_post_g": gain(ks[4], D_MODEL),
        "w_in": nrm(ks[5], (D_MODEL, D_IN), D_MODEL ** -0.5),
        "conv_dw": nrm(ks[6], (CONV_WIDTH, D_CONV), CONV_WIDTH ** -0.5),
        "conv_dw_b": nrm(ks[7], (D_CONV,), 0.02),
        "conv_ln_g": gain(ks[8], D_CONV),
        "conv_ln_b": nrm(ks[9], (D_CONV,), 0.02),
        "w_conv_pw": nrm(ks[10], (D_CONV, D_CONV), D_CONV ** -0.5),
        "attn_sinks": nrm(ks[11], (N_Q_HEADS,), 0.5),
        "rel_bias": nrm(ks[12], (N_BUCKETS, N_Q_HEADS), 0.5),
        "w_out": nrm(ks[13], (D_MIX, D_MODEL), D_MIX ** -0.5),
        "w_gate": nrm(ks[14], (D_MODEL, D_FF), D_MODEL ** -0.5),
        "w_up": nrm(ks[15], (D_MODEL, D_FF), D_MODEL ** -0.5),
        "w_down": nrm(ks[16], (D_FF, D_MODEL), D_FF ** -0.5),
    }


def reference(x, mix_pre_g, mix_post_g, ffn_pre_g, ffn_post_g, w_in, conv_dw, conv_dw_b,
              conv_ln_g, conv_ln_b, w_conv_pw, attn_sinks, rel_bias, w_out,
              w_gate, w_up, w_down):
    B, S, _ = x.shape
    h = x
    for _layer in range(DEPTH):
        hn = rms_norm(h, mix_pre_g)
        proj = hn @ w_in
        q = proj[..., :D_Q].reshape(B, S, N_Q_HEADS, HEAD_DIM)
        k = proj[..., D_Q:D_Q + D_KV].reshape(B, S, N_KV_HEADS, HEAD_DIM)
        v = proj[..., D_Q + D_KV:D_Q + 2 * D_KV].reshape(B, S, N_KV_HEADS, HEAD_DIM)
        u = proj[..., D_Q + 2 * D_KV:]
        attn_out = sliding_window_attention(q, k, v, attn_sinks, rel_bias)
        conv_out = conformer_conv(u, conv_dw, conv_dw_b, conv_ln_g, conv_ln_b, w_conv_pw)
        mixed = jnp.concatenate([attn_out, conv_out], axis=-1) @ w_out
        h = h + rms_norm(mixed, mix_post_g)
        hn = rms_norm(h, ffn_pre_g)
        ff = (jax.nn.silu(hn @ w_gate) * (hn @ w_up)) @ w_down
        h = h + rms_norm(ff, ffn_post_g)
    return h
```

```python
import math
from contextlib import ExitStack
import numpy as np
import concourse.bass as bass
import concourse.mybir as mybir
from concourse.bass_utils import run_bass_kernel_spmd

F32 = mybir.dt.float32
BF16 = mybir.dt.bfloat16
ALU = mybir.AluOpType
AF = mybir.ActivationFunctionType

NCORES = 8
D = 1024
TOK = 4096
TT = 512
NT = TOK // TT
NFC = 22
NRING = 9
MASK = -30000.0
EPS = 1e-6
DEBUG = False
SWDGE_DEPTH = 3
CONV_AHEAD = 6
N_DIRECT = 0


class Op:
    __slots__ = ("eng", "fn", "deps", "dma_sem", "done", "needs_inc", "idx", "phase")


class Sched:
    ENGS = ("pe", "act", "dve", "pool", "sp")

    def __init__(self):
        self.ops = []
        self.lastw = {}
        self.readers = {}

    UNION = ("qT", "sig", "y", "ysq", "sT", "convT", "PT", "attnT", "musq", "rstd", "nmr", "z", "Sb", "atok", "actT", "sg")

    def add(self, eng, fn, reads=(), writes=(), dma_sem=None):
        op = Op()
        op.eng, op.fn, op.deps, op.dma_sem, op.done, op.needs_inc = eng, fn, set(), dma_sem, None, False
        op.idx = len(self.ops)
        op.phase = getattr(self, 'phase', '')
        reads = list(reads)
        writes = list(writes)
        if any((k[0] if isinstance(k, tuple) else k) in self.UNION for k in reads + writes) and "UNI" not in reads and "UNI" not in writes:
            reads.append("UNI")
        for k in reads:
            w = self.lastw.get(k)
            if w is not None:
                op.deps.add(w)
        for k in writes:
            w = self.lastw.get(k)
            if w is not None:
                op.deps.add(w)
            for r in self.readers.get(k, ()):
                op.deps.add(r)
        for k in reads:
            self.readers.setdefault(k, []).append(op)
        for k in writes:
            self.lastw[k] = op
            self.readers[k] = []
        op.deps.discard(op)
        if dma_sem is not None and eng == "pool":
            q = self.__dict__.setdefault("_poolq", [])
            if len(q) >= SWDGE_DEPTH:
                op.deps.add(q[-SWDGE_DEPTH])
            q.append(op)
        if eng == "pe" and dma_sem is None:
            op.deps = {d for d in op.deps if not (d.eng == "pe" and d.dma_sem is None)}
        best = {}
        keep = set()
        for d in op.deps:
            if d.dma_sem is not None:
                keep.add(d)
            elif d.eng not in best or best[d.eng].idx < d.idx:
                best[d.eng] = d
        op.deps = keep | set(best.values())
        for d in op.deps:
            d.needs_inc = True
        self.ops.append(op)
        return op

    def emit(self, nc, es, block):
        sems = {e: es.enter_context(nc.semaphore("sem_" + e)) for e in self.ENGS}
        cnt = {e: 0 for e in self.ENGS}
        dcnt = {}
        for op in self.ops:
            if op.dma_sem is not None:
                nm = op.dma_sem[0]
                dcnt[nm] = dcnt.get(nm, 0) + 16
                op.done = (nm, op.dma_sem[1], dcnt[nm])
            elif op.needs_inc:
                cnt[op.eng] += 1
                op.done = ("sem_" + op.eng, sems[op.eng], cnt[op.eng])
        per = {e: [o for o in self.ops if o.eng == e] for e in self.ENGS}

        def body(ops):
            def run(e):
                waited = {}
                for op in ops:
                    need = {}
                    for d in op.deps:
                        nm, s, v = d.done
                        if waited.get(nm, 0) < v and need.get(nm, (None, 0))[1] < v:
                            need[nm] = (s, v)
                    for nm, (s, v) in need.items():
                        e.wait_ge(s, v)
                        waited[nm] = v
                    if op.fn is None:
                        continue
                    ins = op.fn(e)
                    if op.done is not None:
                        ins.then_inc(op.done[1], 16 if op.dma_sem is not None else 1)
            return run

        block.tensor(body(per["pe"]))
        block.scalar(body(per["act"]))
        block.vector(body(per["dve"]))
        block.gpsimd(body(per["pool"]))
        block.sync(body(per["sp"]))


def build_program(nt=NT, maxops=None, marks=None):
    nc = bass.Bass("TRN2", target_bir_lowering=False)
    S = Sched()
    NTOKX = TOK + 128
    x_d = nc.dram_tensor("x", [NTOKX, D], F32, kind="ExternalInput").ap()
    small_d = nc.dram_tensor("small", [128, 168], F32, kind="ExternalInput").ap()
    ident_d = nc.dram_tensor("ident", [128, 128], F32, kind="ExternalInput").ap()
    gpost_d = nc.dram_tensor("gpost", [128, 2048], F32, kind="ExternalInput").ap()
    bias_d = nc.dram_tensor("biast", [128, 2048], F32, kind="ExternalInput").ap()
    win_d = nc.dram_tensor("win", [14, 128, 1024], F32, kind="ExternalInput").ap()
    wgu_d = nc.dram_tensor("wgu", [2 * NFC, 128, 1024], F32, kind="ExternalInput").ap()
    wdn_d = nc.dram_tensor("wdn", [NFC, 128, 1024], F32, kind="ExternalInput").ap()
    wout_d = nc.dram_tensor("wout", [8, 128, 1024], F32, kind="ExternalInput").ap()
    wpw_d = nc.dram_tensor("wpw", [128, 2048], F32, kind="ExternalInput").ap()
    out_d = nc.dram_tensor("out", [TOK, D], F32, kind="ExternalOutput").ap()
    wscr_d = nc.dram_tensor("wscr", [80, 128, 1024], BF16, kind="Internal").ap()
    if DEBUG:
        dbg_d = nc.dram_tensor("dbg", [128, 8192], F32, kind="ExternalOutput").ap()

    with ExitStack() as es:
        def sb(name, shape, dt):
            return es.enter_context(nc.sbuf_tensor("sb_" + name, shape, dt))

        semc = [0]

        def newsem(name):
            semc[0] += 1
            return (name, es.enter_context(nc.semaphore(name)))

        small = sb("small", [128, 168], F32)
        ident_f = sb("ident_f", [128, 128], F32)
        ident_b = sb("ident_b", [128, 128], BF16)
        ones_m = sb("ones_m", [128, 128], BF16)
        mhalf = sb("mhalf", [128, 4], F32)
        diag = sb("diag", [128, 4, 31, 128], BF16)
        gpost = sb("gpost", [128, 2, 1024], F32)
        biast = sb("biast", [128, 4, 512], F32)
        wout = sb("wout", [128, 8, 1024], BF16)
        wpw = sb("wpw", [128, 4, 512], BF16)
        ring = [sb("ring%d" % i, [128, 1024], BF16) for i in range(NRING)]
        ring_sem = [newsem("rs%d" % i) for i in range(NRING)]
        ring_sem_sw = [newsem("rw%d" % i) for i in range(NRING)]
        xh = [sb("xh%d" % i, [128, 4, 1024], F32) for i in range(2)]
        xh_sem = [[newsem("xs%d_%d" % (i, s)) for s in range(4)] for i in range(2)]
        xs_m = sb("xs_m", [128, 4, 1024], BF16)
        xs_f = sb("xs_f", [128, 2, 1024], BF16)
        hnT_m = sb("hnT_m", [128, 8, 512], BF16)
        hnT_f = sb("hnT_f", [128, 8, 512], BF16)
        kT = [sb("kT%d" % i, [128, 512], BF16) for i in range(2)]
        Va = [sb("Va%d" % i, [128, 4, 2, 128], BF16) for i in range(2)]
        hbuf = sb("hbuf", [128, 4, 544], BF16)
        stat = sb("stat", [128, 64], F32)
        sgh = sb("sgh", [128, 32], BF16)
        UBYTES = 50 * 1024
        uni = sb("uni", [128, UBYTES // 2], BF16)
        uoff = [0]

        def carve(nbytes):
            o = uoff[0]
            uoff[0] += nbytes
            assert uoff[0] <= UBYTES
            return o // 2

        def ubf(shape):
            n = int(np.prod(shape))
            o = carve(n * 2)
            ap = uni[:, o:o + n]
            return ap

        def uf32(n):
            o = carve(n * 4)
            return uni[:, o:o + 2 * n].bitcast(F32)

        qT = ubf([4, 512])
        sig = [ubf([512]) for _ in range(2)]
        ybuf = ubf([4, 512])
        ysq = [ubf([512]) for _ in range(4)]
        sT = ubf([4, 512])
        convT = ubf([4, 512])
        PT = [ubf([4, 512]) for _ in range(2)]
        attnT = ubf([4, 512])
        musq = uf32(512)
        rstd = uf32(512)
        nmr = uf32(512)
        zt = [uf32(512) for _ in range(2)]
        Sb01 = uf32(1024)
        Sb = [Sb01[:, 0:512], Sb01[:, 512:1024]]
        attn_tok = [ubf([512]) for _ in range(2)]
        mixer_end = uoff[0]
        t1m = [(qT, [("qT", j) for j in range(4)]), (PT[0], [("PT", 0, i) for i in range(4)]),
               (PT[1], [("PT", 1, i) for i in range(4)]), (None, [("Sb", 0), ("Sb", 1)])]
        uoff[0] = 0
        actT = ubf([NFC, 512])
        sgt = [ubf([512]) for _ in range(2)]
        assert uoff[0] <= UBYTES and mixer_end <= UBYTES
        t1f = [(actT[:, i * 2048:(i + 1) * 2048], [("actT", 4 * i + k) for k in range(4)]) for i in range(4)]

        ps_all = es.enter_context(nc.psum_tensor("ps_all", [128, 4096], F32))

        def bank(i, n=1):
            return ps_all[:, i * 512:(i + n) * 512]

        bankp = [0]

        reserved = set()
        cool = {}

        def alloc1():
            for k in list(cool):
                cool[k] -= 1
                if cool[k] <= 0:
                    del cool[k]
            while (bankp[0] % 8) in reserved or (bankp[0] % 8) in cool:
                bankp[0] += 1
            b = bankp[0] % 8
            bankp[0] += 1
            return b

        def alloc2():
            while True:
                if bankp[0] % 2:
                    bankp[0] += 1
                b = bankp[0] % 8
                if b in reserved or (b + 1) in reserved:
                    bankp[0] += 2
                    continue
                bankp[0] += 2
                return b

        def PS(b, n=1):
            return [("ps", b + i) for i in range(n)]

        def mark(name):
            S.phase = name
            if marks is not None:
                marks.append((name, len(S.ops)))

        def PE(fn, r=(), w=()):
            return S.add("pe", fn, r, w)

        def ACT(fn, r=(), w=()):
            return S.add("act", fn, r, w)

        def DVE(fn, r=(), w=()):
            return S.add("dve", fn, r, w)

        def POOL(fn, r=(), w=()):
            return S.add("pool", fn, r, w)

        def ld(queue, dst, src, key):
            sem = newsem("ld_" + key)
            S.add(queue, lambda e, d=dst, s=src: e.dma_start(out=d, in_=s), (), [key], dma_sem=sem)

        ld("sp", small[:], small_d, "small")
        ld("sp", ident_f[:], ident_d, "ident_f")
        WOUT = ["wout%d" % kc for kc in range(8)]
        WPW = ["wpw0", "wpw1"]

        POOL(lambda e: e.memset(mhalf[:], -0.5), (), ["mhalf"])
        POOL(lambda e: e.memset(stat[:, 63:64], 1.0), (), [("stat", 63)])
        POOL(lambda e: e.memset(ones_m[:], 1.0 / 512.0), (), ["ones_m"])
        for i in range(2):
            POOL(lambda e, i=i: e.memset(Va[i][:, :, :, 64:128], 1.0), (), [("Va", i)])
        POOL(lambda e: e.memset(hbuf[:, :, 0:32], 0.0), (), [("hbuf", c) for c in range(4)])
        DVE(lambda e: e.tensor_copy(out=ident_b[:], in_=ident_f[:]), ["ident_f"], ["ident_b"])
        ACT(lambda e: e.activation(out=small[:, 154:162], in_=small[:, 154:162], func=AF.Exp), ["small"], ["small"])

        ring_state = {"n": 0}

        conv_sem = [newsem("cv%d" % i) for i in range(8)]
        conv_state = {"n": 0, "done": set()}

        def convert(slot):
            if slot in conv_state["done"]:
                return
            conv_state["done"].add(slot)
            n = conv_state["n"]
            conv_state["n"] += 1
            if slot < 14:
                src = win_d[slot]
            elif slot < 58:
                src = wgu_d[slot - 14]
            else:
                src = wdn_d[slot - 58]
            S.add("pool", lambda e, slot=slot, src=src: e.dma_start(out=wscr_d[slot], in_=src), (), [("scr", slot)], dma_sem=conv_sem[n % 8])

        def ring_load(slot):
            i = ring_state["n"] % NRING
            ring_state["n"] += 1
            if ring_state["n"] <= N_DIRECT:
                S.add("pool", lambda e, i=i, slot=slot: e.dma_start(out=ring[i][:], in_=win_d[slot]), (), [("ring", i)], dma_sem=ring_sem_sw[i])
                return i
            S.add("sp", lambda e, i=i, slot=slot: e.dma_start(out=ring[i][:], in_=wscr_d[slot]), [("scr", slot)], [("ring", i)], dma_sem=ring_sem[i])
            return i

        stream = []
        halo_slots = [4, 5, 6, 10, 7, 11, 8, 12, 9, 13]
        front_slots = [4, 5, 6, 10, 7, 11, 8, 12, 9, 13]
        for sl in front_slots:
            stream.append(sl)
        for t in range(nt):
            for sl in (0, 1, 2, 3):
                stream.append(sl)
            if t + 1 < nt:
                for sl in front_slots:
                    stream.append(sl)
            for fc in range(NFC):
                stream.append(14 + 2 * fc)
                stream.append(14 + 2 * fc + 1)
            for fc in range(NFC):
                stream.append(58 + fc)
        sp_ = {"issued": 0, "used": 0}

        def prefetch():
            while sp_["issued"] < len(stream) and sp_["issued"] < sp_["used"] + NRING:
                if sp_["issued"] >= N_DIRECT:
                    for la in range(sp_["issued"], min(len(stream), sp_["issued"] + CONV_AHEAD)):
                        convert(stream[la])
                ring_load(stream[sp_["issued"]])
                sp_["issued"] += 1

        def next_w():
            i = sp_["used"] % NRING
            sp_["used"] += 1
            return ring[i], ("ring", i)

        mark("setup_done")

        def load_x(par, s, row0):
            S.add("sp", lambda e: e.dma_start(out=xh[par][:, s, :], in_=x_d[row0:row0 + 128, :]), (), [("xh", par, s)],
                  dma_sem=xh_sem[par][s])

        def store_out(par, s, row0):
            S.add("sp", lambda e: e.dma_start(out=out_d[row0:row0 + 128, :], in_=xh[par][:, s, :]), [("xh", par, s)],
                  [("out", row0)], dma_sem=xh_sem[par][s])

        stat_i = [0]

        def stat_col(n=1):
            c = stat_i[0]
            stat_i[0] += 2
            if stat_i[0] >= 60:
                stat_i[0] = 0
            return c

        NOPOOL = [False]

        def rms_rstd(src_ap, src_keys, width, extra_w=(), junk_ap=None, junk_key=()):
            c = stat_col(2)
            kss = ("stat", c)
            kr = ("stat", c + 1)
            ACT(lambda e: e.activation(out=junk_ap, in_=src_ap, func=AF.Square, accum_out=stat[:, c:c + 1]),
                list(src_keys), list(junk_key) + [kss] + list(extra_w))
            DVE(lambda e: e.tensor_scalar(out=stat[:, c:c + 1], in0=stat[:, c:c + 1], scalar1=1.0 / width, scalar2=EPS,
                                          op0=ALU.mult, op1=ALU.add), [kss], [kss])
            if NOPOOL[0]:
                ACT(lambda e: e.activation(out=stat[:, c + 1:c + 2], in_=stat[:, c:c + 1], func=AF.Sqrt), [kss], [kr])
                DVE(lambda e: e.reciprocal(out=stat[:, c + 1:c + 2], in_=stat[:, c + 1:c + 2]), [kr], [kr])
            else:
                POOL(lambda e: e.tensor_tensor(out=stat[:, c + 1:c + 2], in0=stat[:, c:c + 1], in1=mhalf[:, 0:1], op=ALU.pow),
                     [kss, "mhalf"], [kr])
            return stat[:, c + 1:c + 2], kr

        def xs_slot(s, ph):
            if ph == "m":
                return xs_m[:, s, :], [("xsm", s)]
            if s < 2:
                return xs_f[:, s, :], [("xsf", s)]
            return ybuf[:, (s - 2) * 1024:(s - 1) * 1024], [("y", 2 * (s - 2)), ("y", 2 * (s - 2) + 1)]

        def prenorm_a(par, s, ph):
            xap, kx = xs_slot(s, ph)
            src = xh[par][:, s, :]
            return rms_rstd(src, [("xh", par, s)], 1024, junk_ap=xap, junk_key=kx)

        def prenorm_b(par, s, ph, rk):
            xap, kx = xs_slot(s, ph)
            src = xh[par][:, s, :]
            r_ap, kr = rk
            ACT(lambda e: e.activation(out=xap, in_=src, func=AF.Copy, scale=r_ap), [("xh", par, s), kr], kx)

        def prenorm(par, s, ph):
            prenorm_b(par, s, ph, prenorm_a(par, s, ph))

        def transposes(s, ncols, gcol0, dst_col0, ph):
            hn = hnT_m if ph == "m" else hnT_f
            xap, kx = xs_slot(s, ph)
            if ph == "f":
                b = alloc1()
                pb = bank(b).bitcast(BF16)
                for kc in range(8):
                    PE(lambda e, kc=kc: e.transpose(pb[:, kc * 128:kc * 128 + ncols], xap[:, kc * 128:(kc + 1) * 128], ident_b[:]),
                       kx + ["ident_b"], PS(b))
                for kc in range(8):
                    DVE(lambda e, kc=kc: e.tensor_scalar(out=hn[:, kc, dst_col0:dst_col0 + ncols], in0=pb[:, kc * 128:kc * 128 + ncols],
                                                         scalar1=small[:, gcol0 + kc:gcol0 + kc + 1], scalar2=None, op0=ALU.mult),
                        PS(b) + ["small"], [("hnT" + ph, kc)])
                return
            bb = [alloc1(), alloc1()]
            pbs = [bank(bb[0]).bitcast(BF16), bank(bb[1]).bitcast(BF16)]
            for par_ in range(2):
                for kc in range(par_, 8, 2):
                    PE(lambda e, kc=kc, par_=par_: e.transpose(pbs[par_][:, (kc // 2) * 128:(kc // 2) * 128 + ncols], xap[:, kc * 128:(kc + 1) * 128], ident_b[:]),
                       kx + ["ident_b"], PS(bb[par_]))
            for kc in range(0, 8, 2):
                DVE(lambda e, kc=kc: e.tensor_scalar(out=hn[:, kc, dst_col0:dst_col0 + ncols], in0=pbs[0][:, (kc // 2) * 128:(kc // 2) * 128 + ncols],
                                                     scalar1=small[:, gcol0 + kc:gcol0 + kc + 1], scalar2=None, op0=ALU.mult),
                    PS(bb[0]) + ["small"], [("hnT" + ph, kc)])
            for kc in range(1, 8, 2):
                ACT(lambda e, kc=kc: e.activation(out=hn[:, kc, dst_col0:dst_col0 + ncols], in_=pbs[1][:, (kc // 2) * 128:(kc // 2) * 128 + ncols],
                                                  func=AF.Copy, scale=small[:, gcol0 + kc:gcol0 + kc + 1]),
                    PS(bb[1]) + ["small"], [("hnT" + ph, kc)])

        def norm_transpose(par, s, ncols, gcol0, dst_col0, ph):
            prenorm(par, s, ph)
            transposes(s, ncols, gcol0, dst_col0, ph)

        def fence(tag):
            mark("fence_" + tag)
            POOL(lambda e: e.memset(stat[:, 62:63], 0.0), ["UNI"], ["UNI"])

        HN = [("hnT", kc) for kc in range(8)]

        def proj_fm(wt, wk, ncols, ph="m"):
            hn = hnT_m if ph == "m" else hnT_f
            b = alloc1()
            for kc in range(8):
                PE(lambda e, kc=kc: e.matmul(bank(b)[:, 0:ncols], lhsT=wt[:, kc * 128:(kc + 1) * 128], rhs=hn[:, kc, 0:ncols],
                                             start=(kc == 0), stop=(kc == 7)), [wk, ("hnT" + ph, kc)], PS(b))
            return b

        def w_kv(par, ncols, halo, with_halo=False):
            nsub = ncols // 128
            wt, wk = next_w()
            b = proj_fm(wt, wk, ncols)
            if with_halo:
                bh = proj_fm(wt, wk, 128, "f")
                ACT(lambda e, bh=bh: e.activation(out=kT[1 - par][:, 384:512], in_=bank(bh)[:, 0:128], func=AF.Copy), PS(bh), [("kT", 1 - par)])
            if halo:
                ACT(lambda e, b=b: e.activation(out=kT[par][:, 384:512], in_=bank(b)[:, 0:128], func=AF.Copy), PS(b), [("kT", par)])
            else:
                ACT(lambda e, b=b: e.activation(out=kT[par][:, :], in_=bank(b), func=AF.Copy), PS(b), [("kT", par)])
            prefetch()
            wt, wk = next_w()
            b = alloc1()
            for s in range(nsub):
                for kc in range(8):
                    PE(lambda e, kc=kc, s=s, b=b, wt=wt: e.matmul(bank(b)[:, s * 128:(s + 1) * 128], lhsT=hnT_m[:, kc, s * 128:(s + 1) * 128],
                                                      rhs=wt[:, kc * 128:(kc + 1) * 128], start=(kc == 0), stop=(kc == 7)),
                       [wk, ("hnTm", kc)], PS(b))
            if halo:
                ACT(lambda e, b=b: e.activation(out=Va[par][:, 3, :, 0:64], in_=bank(b)[:, 0:128].rearrange("p (g d) -> p g d", g=2), func=AF.Copy),
                    PS(b), [("Va", par)])
            else:
                ACT(lambda e, b=b: e.activation(out=Va[par][:, :, :, 0:64], in_=bank(b).rearrange("p (s g d) -> p s g d", s=4, g=2), func=AF.Copy),
                    PS(b), [("Va", par)])
            if with_halo:
                bh = alloc1()
                for kc in range(8):
                    PE(lambda e, kc=kc, bh=bh, wt=wt: e.matmul(bank(bh)[:, 0:128], lhsT=hnT_f[:, kc, 0:128],
                                                               rhs=wt[:, kc * 128:(kc + 1) * 128], start=(kc == 0), stop=(kc == 7)),
                       [wk, ("hnTf", kc)], PS(bh))
                ACT(lambda e, bh=bh: e.activation(out=Va[1 - par][:, 3, :, 0:64], in_=bank(bh)[:, 0:128].rearrange("p (g d) -> p g d", g=2), func=AF.Copy),
                    PS(bh), [("Va", 1 - par)])
            prefetch()

        def w_q():
            for j in range(4):
                wt, wk = next_w()
                b = proj_fm(wt, wk, 512)
                ACT(lambda e, j=j, b=b: e.activation(out=qT[:, j * 512:(j + 1) * 512], in_=bank(b), func=AF.Copy), PS(b) + ["UNI"], [("qT", j)])
                prefetch()

        def w_ag(ncols, halo, chunks=(0, 1, 2, 3), with_halo=False):
            for c in chunks:
                wt, wk = next_w()
                ba = proj_fm(wt, wk, ncols)
                if with_halo:
                    bah = proj_fm(wt, wk, 128, "f")
                prefetch()
                wt, wk = next_w()
                bg = proj_fm(wt, wk, ncols)
                if with_halo:
                    bgh = proj_fm(wt, wk, 128, "f")
                    ACT(lambda e, bgh=bgh: e.activation(out=sgh[:, 0:32], in_=bank(bgh)[:, 96:128], func=AF.Tanh, scale=0.5), PS(bgh), ["sgh"])
                    DVE(lambda e, bah=bah, c=c: e.scalar_tensor_tensor(out=hbuf[:, c, 0:32], in0=sgh[:, 0:32], scalar=1.0, in1=bank(bah)[:, 96:128],
                                                                     op0=ALU.add, op1=ALU.mult), PS(bah) + ["sgh"], [("hbuf", c)])
                prefetch()
                if halo:
                    ACT(lambda e, bg=bg: e.activation(out=sig[0][:, 0:32], in_=bank(bg)[:, 96:128], func=AF.Tanh, scale=0.5), PS(bg), [("sig", 0)])
                    DVE(lambda e, ba=ba, c=c: e.scalar_tensor_tensor(out=hbuf[:, c, 0:32], in0=sig[0][:, 0:32], scalar=1.0, in1=bank(ba)[:, 96:128],
                                                                    op0=ALU.add, op1=ALU.mult), PS(ba) + [("sig", 0)], [("hbuf", c)])
                else:
                    sg = sig[c % 2]
                    ACT(lambda e, bg=bg, sg=sg: e.activation(out=sg, in_=bank(bg), func=AF.Tanh, scale=0.5), PS(bg) + ["UNI"], [("sig", c % 2)])
                    DVE(lambda e, ba=ba, c=c, sg=sg: e.scalar_tensor_tensor(out=hbuf[:, c, 32:544], in0=sg, scalar=1.0, in1=bank(ba),
                                                                           op0=ALU.add, op1=ALU.mult), PS(ba) + [("sig", c % 2)], [("hbuf", c)])

        def attn_scores(t, bl, par):
            first = (t == 0 and bl == 0)
            ptb = PT[bl % 2]
            for kt in range(2):
                if kt == 1:
                    ksrc, kcol, kkey = kT[par], bl * 128, ("kT", par)
                elif bl > 0:
                    ksrc, kcol, kkey = kT[par], (bl - 1) * 128, ("kT", par)
                else:
                    ksrc, kcol, kkey = kT[1 - par], 384, ("kT", 1 - par)
                for g in range(2):
                    b = alloc1()
                    q_rhs = qT[g * 64:(g + 1) * 64, :].rearrange("p (j t) -> p j t", j=4)[:, :, bl * 128:(bl + 1) * 128]
                    PE(lambda e, b=b, g=g, ksrc=ksrc, kcol=kcol, q_rhs=q_rhs: e.matmul(
                        bank(b).rearrange("p (j t) -> p j t", j=4), lhsT=ksrc[g * 64:(g + 1) * 64, kcol:kcol + 128], rhs=q_rhs,
                        start=True, stop=True), [kkey] + [("qT", j) for j in range(4)], PS(b))
                    sbuf_ = Sb[(kt * 2 + g) % 2]
                    bt = biast[:, (2 * (1 - kt)) + g, :]
                    DVE(lambda e, b=b, sbuf_=sbuf_, bt=bt: e.scalar_tensor_tensor(out=sbuf_, in0=bank(b), scalar=0.125, in1=bt,
                                                                                  op0=ALU.mult, op1=ALU.add),
                        PS(b) + ["biast", "UNI"], [("Sb", (kt * 2 + g) % 2)])
                    pslice = ptb[:, (kt * 2 + g) * 512:(kt * 2 + g + 1) * 512]
                    pk = ("PT", bl % 2, kt * 2 + g)
                    ACT(lambda e, sbuf_=sbuf_, pslice=pslice: e.activation(out=pslice, in_=sbuf_, func=AF.Exp),
                        [("Sb", (kt * 2 + g) % 2)], [pk])
                    if first and kt == 0:
                        DVE(lambda e, pslice=pslice: e.tensor_scalar(out=pslice, in0=pslice, scalar1=small[:, 152:153], scalar2=None,
                                                                     op0=ALU.mult), [pk, "small"], [pk])

        def attn_pv(t, bl, par):
            ptb = PT[bl % 2]
            atok = attn_tok[bl % 2]
            for g in range(2):
                b = alloc1()
                for j in range(4):
                    for kt in range(2):
                        if kt == 1:
                            vsrc, vkey = Va[par][:, bl, g, 0:65], ("Va", par)
                        elif bl > 0:
                            vsrc, vkey = Va[par][:, bl - 1, g, 0:65], ("Va", par)
                        else:
                            vsrc, vkey = Va[1 - par][:, 3, g, 0:65], ("Va", 1 - par)
                        pslice = ptb[:, (kt * 2 + g) * 512 + j * 128:(kt * 2 + g) * 512 + (j + 1) * 128]
                        PE(lambda e, b=b, vsrc=vsrc, pslice=pslice, kt=kt, j=j: e.matmul(bank(b)[:, j * 65:(j + 1) * 65], lhsT=pslice, rhs=vsrc,
                                                                                         start=(kt == 0), stop=(kt == 1)),
                           [vkey, ("PT", bl % 2, kt * 2 + g)], PS(b))
                c = stat_col()
                stat_col()
                kd = ("stat", c)
                ov = bank(b)[:, 0:260].rearrange("p (j d) -> p j d", j=4)
                DVE(lambda e, ov=ov, c=c, g=g: e.tensor_tensor(out=stat[:, c:c + 4], in0=ov[:, :, 64], in1=small[:, 154 + 4 * g:158 + 4 * g], op=ALU.add),
                    PS(b) + ["small"], [kd, ("stat", c + 1), ("stat", c + 2), ("stat", c + 3)])
                DVE(lambda e, c=c: e.reciprocal(out=stat[:, c:c + 4], in_=stat[:, c:c + 4]), [kd], [kd, ("stat", c + 1), ("stat", c + 2), ("stat", c + 3)])
                for j in range(4):
                    hh = 4 * g + j
                    DVE(lambda e, ov=ov, c=c, j=j, hh=hh, atok=atok: e.tensor_scalar(out=atok[:, hh * 64:(hh + 1) * 64], in0=ov[:, j, 0:64],
                                                                                     scalar1=stat[:, c + j:c + j + 1], scalar2=None, op0=ALU.mult),
                        PS(b) + [kd], [("atok", bl % 2, g)])

        def attn_T(bl):
            atok = attn_tok[bl % 2]
            b = alloc1()
            pb = bank(b).bitcast(BF16)
            for cc in range(4):
                PE(lambda e, cc=cc, pb=pb, atok=atok: e.transpose(pb[:, cc * 128:(cc + 1) * 128], atok[:, cc * 128:(cc + 1) * 128], ident_b[:]),
                   [("atok", bl % 2, 0), ("atok", bl % 2, 1), "ident_b"], PS(b))
            dst = attnT.rearrange("p (c t) -> p c t", c=4)[:, :, bl * 128:(bl + 1) * 128]
            ACT(lambda e, pb=pb, dst=dst: e.activation(out=dst, in_=pb[:, 0:512].rearrange("p (c t) -> p c t", c=4), func=AF.Copy),
                PS(b) + ["UNI"], [("attnT", bl)])

        def conv_chunk(c):
            b = alloc1()
            for tau in range(31):
                PE(lambda e, tau=tau: e.matmul(bank(b), lhsT=diag[:, c, tau, :], rhs=hbuf[:, c, 2 + tau:2 + tau + 512],
                                               start=(tau == 0), stop=(tau == 30)), [("diag", c, tau), ("hbuf", c)], PS(b))
            ACT(lambda e: e.activation(out=ybuf[:, c * 512:(c + 1) * 512], in_=bank(b), func=AF.Identity, bias=small[:, 124 + c:125 + c]),
                PS(b) + ["small", "UNI"], [("y", c)])
            ACT(lambda e: e.activation(out=ysq[c], in_=bank(b), func=AF.Square, bias=small[:, 124 + c:125 + c]),
                PS(b) + ["small", "UNI"], [("ysq", c)])

        def mixer_tile(t, par):
            fence("m")
            w_q()
            mark("q_done")
            conv_chunk(0)
            for c in range(4):
                attn_scores(t, c, par)
                if c < 3:
                    conv_chunk(c + 1)
                if c > 0 and c < 3:
                    attn_T(c - 1)
                if c < 3:
                    attn_pv(t, c, par)
                if t + 1 < nt:
                    prenorm(1 - par, c, "m")
                    if c > 0:
                        transposes(c - 1, 128, 136, (c - 1) * 128, "m")
            mark("win_done")
            POOL(lambda e: e.tensor_copy(out=hbuf[:, :, 0:32], in_=hbuf[:, :, 512:544]), [("hbuf", c) for c in range(4)], [("hbuf", c) for c in range(4)])
            mark("attnconv_done")
            b_mu = alloc1()
            b_e2 = alloc1()
            reserved.update((b_mu, b_e2))
            for c in range(4):
                PE(lambda e, c=c: e.matmul(bank(b_mu), lhsT=ones_m[:], rhs=ybuf[:, c * 512:(c + 1) * 512], start=(c == 0), stop=(c == 3)),
                   ["ones_m", ("y", c)], PS(b_mu))
            for c in range(4):
                PE(lambda e, c=c: e.matmul(bank(b_e2), lhsT=ones_m[:], rhs=ysq[c], start=(c == 0), stop=(c == 3)),
                   ["ones_m", ("ysq", c)], PS(b_e2))
            attn_T(2)
            attn_pv(t, 3, par)
            if t + 1 < nt:
                transposes(3, 128, 136, 384, "m")
            attn_T(3)
            ACT(lambda e: e.activation(out=stat[:, 63:64], in_=stat[:, 63:64], func=AF.Sqrt), [("stat", 63)], [("stat", 63)])
            ACT(lambda e: e.activation(out=musq, in_=bank(b_mu), func=AF.Square), PS(b_mu) + ["UNI"], ["musq"])
            DVE(lambda e: e.tensor_tensor(out=rstd, in0=bank(b_e2), in1=musq, op=ALU.subtract), PS(b_e2) + ["musq"], ["rstd"])
            ACT(lambda e: e.activation(out=rstd, in_=rstd, func=AF.Sqrt, bias=small[:, 153:154]), ["rstd", "small"], ["rstd"])
            ACT(lambda e: e.activation(out=stat[:, 63:64], in_=stat[:, 63:64], func=AF.Silu), [("stat", 63)], [("stat", 63)])
            DVE(lambda e: e.reciprocal(out=rstd, in_=rstd), ["rstd"], ["rstd"])
            DVE(lambda e: e.scalar_tensor_tensor(out=nmr, in0=bank(b_mu), scalar=-1.0, in1=rstd, op0=ALU.mult, op1=ALU.mult),
                PS(b_mu) + ["rstd"], ["nmr"])
            reserved.clear()
            cool[b_mu] = 6
            cool[b_e2] = 6
            for c in range(4):
                z = zt[c % 2]
                DVE(lambda e, c=c, z=z: e.tensor_tensor(out=z, in0=ybuf[:, c * 512:(c + 1) * 512], in1=rstd, op=ALU.mult),
                    [("y", c), "rstd"], [("z", c % 2)])
                DVE(lambda e, z=z: e.tensor_tensor(out=z, in0=z, in1=nmr, op=ALU.add), [("z", c % 2), "nmr"], [("z", c % 2)])
                ACT(lambda e, c=c, z=z: e.activation(out=sT[:, c * 512:(c + 1) * 512], in_=z, func=AF.Silu,
                                                     scale=small[:, 128 + c:129 + c], bias=small[:, 132 + c:133 + c]),
                    [("z", c % 2), "small"], [("sT", c)])
            mark("ln_done")
            if t + 1 < nt:
                w_kv(1 - par, 512, False)
                w_ag(512, False, (0,))
            for co in range(4):
                b = alloc1()
                for ci in range(4):
                    PE(lambda e, b=b, co=co, ci=ci: e.matmul(bank(b), lhsT=wpw[:, ci, co * 128:(co + 1) * 128], rhs=sT[:, ci * 512:(ci + 1) * 512],
                                                             start=(ci == 0), stop=(ci == 3)), WPW + [("sT", ci)], PS(b))
                ACT(lambda e, b=b, co=co: e.activation(out=convT[:, co * 512:(co + 1) * 512], in_=bank(b), func=AF.Copy), PS(b), [("convT", co)])
            mark("pw_done")
            for s in range(4):
                b = alloc2()
                for hf in range(2):
                    for kc in range(8):
                        if kc < 4:
                            lh = attnT[:, kc * 512 + s * 128:kc * 512 + (s + 1) * 128]
                            lk = ("attnT", s)
                        else:
                            lh = convT[:, (kc - 4) * 512 + s * 128:(kc - 4) * 512 + (s + 1) * 128]
                            lk = ("convT", kc - 4)
                        PE(lambda e, b=b, hf=hf, kc=kc, lh=lh: e.matmul(bank(b + hf), lhsT=lh, rhs=wout[:, kc, hf * 512:(hf + 1) * 512],
                                                                       start=(kc == 0), stop=(kc == 7)), [lk, WOUT[kc]], PS(b + hf))
                post_norm_residual(b, par, s, 0)
            mark("wout_done")
            rks = {}
            rks[0] = prenorm_a(par, 0, "f")
            for s in range(4):
                if s + 1 < 4:
                    rks[s + 1] = prenorm_a(par, s + 1, "f")
                prenorm_b(par, s, "f", rks[s])
            nxt = t + 1 < nt
            if nxt:
                w_ag(512, False, (1,))
            transposes(0, 128, 144, 0, "f")
            transposes(1, 128, 144, 128, "f")
            if nxt:
                w_ag(512, False, (2,))
            transposes(2, 128, 144, 256, "f")
            if nxt:
                w_ag(512, False, (3,))
            transposes(3, 128, 144, 384, "f")

        def post_norm_residual(b, par, s, which):
            bf, keys = (t1m if which == 0 else t1f)[s]
            if bf is None:
                tt = Sb01
                jk = Sb01.bitcast(BF16)[:, 0:1024]
            else:
                tt = bf.bitcast(F32)
                jk = bf[:, 0:1024]
            r_ap, kr = rms_rstd(bank(b, 2), PS(b, 2), 1024, extra_w=[("sq", b)], junk_ap=jk, junk_key=keys)
            DVE(lambda e: e.tensor_tensor(out=tt, in0=bank(b, 2), in1=gpost[:, which, :], op=ALU.mult),
                PS(b, 2) + [("sq", b), "gpost"], keys)
            if which == 1:
                POOL(lambda e: e.tensor_scalar(out=tt, in0=tt, scalar1=r_ap, scalar2=0.0, op0=ALU.mult, op1=ALU.add), keys + [kr], keys)
                POOL(lambda e: e.tensor_tensor(out=xh[par][:, s, :], in0=xh[par][:, s, :], in1=tt, op=ALU.add),
                     keys + [("xh", par, s)], [("xh", par, s)])
            else:
                DVE(lambda e: e.scalar_tensor_tensor(out=xh[par][:, s, :], in0=tt, scalar=r_ap, in1=xh[par][:, s, :], op0=ALU.mult, op1=ALU.add),
                    keys + [kr, ("xh", par, s)], [("xh", par, s)])

        def ffn_tile(t, par):
            row0 = t * TT
            fence("f")
            for fc in range(NFC):
                wg, kg = next_w()
                bg = proj_fm(wg, kg, 512, "f")
                prefetch()
                wu, ku = next_w()
                bu = proj_fm(wu, ku, 512, "f")
                prefetch()
                sg = sgt[fc % 2]
                ACT(lambda e, bg=bg, sg=sg: e.activation(out=sg, in_=bank(bg), func=AF.Silu), PS(bg) + ["UNI"], [("sg", fc % 2)])
                DVE(lambda e, bu=bu, sg=sg, fc=fc: e.tensor_tensor(out=actT[:, fc * 512:(fc + 1) * 512], in0=bank(bu), in1=sg, op=ALU.mult),
                    PS(bu) + [("sg", fc % 2)], [("actT", fc)])
            mark("gateup_done")
            bankp[0] = 0
            for fc in range(NFC):
                wd, kd = next_w()
                for s in range(4):
                    for hf in range(2):
                        PE(lambda e, s=s, hf=hf, fc=fc, wd=wd: e.matmul(bank(2 * s + hf), lhsT=actT[:, fc * 512 + s * 128:fc * 512 + (s + 1) * 128],
                                                                       rhs=wd[:, hf * 512:(hf + 1) * 512], start=(fc == 0), stop=(fc == NFC - 1)),
                           [kd, ("actT", fc)], PS(2 * s + hf))
                prefetch()
            for s in range(4):
                post_norm_residual(2 * s, par, s, 1)
                store_out(par, s, row0 + s * 128)

        load_x(1, 0, 0)
        for s in range(4):
            load_x(0, s, 128 + s * 128)
        prefetch()
        ld("sp", biast[:].rearrange("p a b -> p (a b)"), bias_d, "biast")
        ld("sp", gpost[:].rearrange("p a b -> p (a b)"), gpost_d, "gpost")
        mark("halo_start")
        NOPOOL[0] = True
        prenorm(1, 0, "f")
        for s_ in range(4):
            prenorm(0, s_, "m")
        NOPOOL[0] = False
        transposes(0, 128, 136, 0, "f")
        mark("halo_nt")
        for s_ in range(4):
            transposes(s_, 128, 136, s_ * 128, "m")
        w_kv(0, 512, False, with_halo=True)
        w_ag(512, False, with_halo=True)
        mark("halo_done")
        for c in range(4):
            for tau in range(31):
                eng = DVE if (tau % 2 == 0) else POOL
                eng(lambda e, c=c, tau=tau: e.tensor_scalar(out=diag[:, c, tau, :], in0=ident_f[:],
                                                            scalar1=small[:, c * 31 + tau:c * 31 + tau + 1], scalar2=0.5,
                                                            op0=ALU.mult, op1=ALU.mult),
                    ["ident_f", "small"], [("diag", c, tau)])
        for kc in range(8):
            ld("pool", wout[:, kc, :], wout_d[kc], "wout%d" % kc)
        for i in range(2):
            ld("pool", wpw[:].rearrange("p a b -> p (a b)")[:, i * 1024:(i + 1) * 1024], wpw_d[:, i * 1024:(i + 1) * 1024], "wpw%d" % i)
        for t in range(nt):
            par = t % 2
            if t + 1 < nt:
                for s in range(4):
                    load_x(1 - par, s, 128 + (t + 1) * TT + s * 128)
            mixer_tile(t, par)
            ffn_tile(t, par)
        S.add("sp", None, [("out", r) for r in range(0, nt * TT, 128)], ())

        if maxops is not None:
            S.ops = S.ops[:maxops]
        if marks is not None:
            marks.append(("__sched__", S))
        block = es.enter_context(nc.Block())
        S.emit(nc, es, block)
    return nc


def _t5_bucket(dist):
    dist = np.asarray(dist, dtype=np.int64)
    d = np.maximum(dist, 1).astype(np.float32)
    large = 16 + (np.log(d / np.float32(16)) / np.float32(math.log(128 / 16)) * np.float32(16)).astype(np.int32)
    large = np.minimum(large, 31)
    return np.where(dist < 16, dist, large)


def _prep_shared(inp):
    f = np.float32
    w_in = np.asarray(inp["w_in"], f)
    cols = []
    for j in range(4):
        cols.append(np.concatenate([np.arange(j * 64, (j + 1) * 64), np.arange((4 + j) * 64, (5 + j) * 64)]))
    cols.append(np.arange(512, 640))
    cols.append(np.arange(640, 768))
    for c in range(4):
        cols.append(np.arange(768 + c * 128, 768 + (c + 1) * 128))
    for c in range(4):
        cols.append(np.arange(1280 + c * 128, 1280 + (c + 1) * 128))

    def slotify(w, colidx):
        blk = w[:, colidx]
        return np.ascontiguousarray(blk.reshape(8, 128, 128).transpose(1, 0, 2).reshape(128, 1024))

    win = np.stack([slotify(w_in, c) for c in cols])
    w_gate = np.asarray(inp["w_gate"], f)
    w_up = np.asarray(inp["w_up"], f)
    wgu = np.empty((2 * NFC, 128, 1024), f)
    for fc in range(NFC):
        ci = np.arange(fc * 128, (fc + 1) * 128)
        wgu[2 * fc] = slotify(w_gate, ci)
        wgu[2 * fc + 1] = slotify(w_up, ci)
    wdn = np.ascontiguousarray(np.asarray(inp["w_down"], f).reshape(NFC, 128, 1024))
    w_out = np.asarray(inp["w_out"], f)
    rows = [np.arange(c * 128, (c + 1) * 128) for c in range(8)]
    wout = np.ascontiguousarray(np.stack([w_out[r, :] for r in rows]))
    w_pw = np.asarray(inp["w_conv_pw"], f)
    wpw = np.ascontiguousarray(w_pw.reshape(4, 128, 512).transpose(1, 0, 2).reshape(128, 2048))
    small = np.zeros((128, 168), f)
    dw = np.asarray(inp["conv_dw"], f)
    small[:, 0:124] = dw.reshape(31, 4, 128).transpose(2, 1, 0).reshape(128, 124)
    small[:, 124:128] = np.asarray(inp["conv_dw_b"], f).reshape(4, 128).T
    small[:, 128:132] = np.asarray(inp["conv_ln_g"], f).reshape(4, 128).T
    small[:, 132:136] = np.asarray(inp["conv_ln_b"], f).reshape(4, 128).T
    small[:, 136:144] = np.asarray(inp["mix_pre_g"], f).reshape(8, 128).T
    small[:, 144:152] = np.asarray(inp["ffn_pre_g"], f).reshape(8, 128).T
    small[:, 153] = EPS
    gpost = np.empty((128, 2048), f)
    gpost[:, 0:1024] = np.asarray(inp["mix_post_g"], f)[None, :]
    gpost[:, 1024:2048] = np.asarray(inp["ffn_post_g"], f)[None, :]
    rel = np.asarray(inp["rel_bias"], f)
    k = np.arange(128)[:, None]
    q = np.arange(128)[None, :]
    d_cur = q - k
    d_prev = q + 128 - k
    bc = _t5_bucket(np.maximum(d_cur, 0))
    bp = _t5_bucket(np.minimum(np.maximum(d_prev, 0), 255))
    biast = np.empty((128, 4, 4, 128), f)
    for g in range(2):
        for j in range(4):
            hh = 4 * g + j
            biast[:, g, j, :] = np.where(d_cur >= 0, rel[bc, hh], f(MASK))
            biast[:, 2 + g, j, :] = np.where(d_prev < 128, rel[bp, hh], f(MASK))
    biast = np.ascontiguousarray(biast.reshape(128, 2048))
    small[:, 154:162] = np.asarray(inp["attn_sinks"], f)[None, :]
    ident = np.eye(128, dtype=f)
    return dict(win=win, wgu=wgu, wdn=wdn, wout=wout, wpw=wpw, gpost=gpost, biast=biast, ident=ident), small


_NC_CACHE = {}


def kernel(**inputs):
    x = np.asarray(inputs["x"], np.float32)
    shared, small = _prep_shared(inputs)
    in_maps = []
    for c in range(NCORES):
        b, half = c // 2, c % 2
        xc = np.zeros((TOK + 128, D), np.float32)
        if half == 0:
            xc[128:] = x[b, 0:TOK]
        else:
            xc[:] = x[b, TOK - 128:2 * TOK]
        sm = small.copy()
        sm[:, 152] = 1.0 if half == 1 else 0.0
        m = dict(shared)
        m["x"] = xc
        m["small"] = sm
        in_maps.append(m)
    if "nc" not in _NC_CACHE:
        _NC_CACHE["nc"] = build_program()
    nc = _NC_CACHE["nc"]
    res = run_bass_kernel_spmd(nc, in_maps, core_ids=list(range(NCORES)))
    out = np.empty((4, 8192, D), np.float32)
    for c in range(NCORES):
        b, half = c // 2, c % 2
        out[b, half * TOK:(half + 1) * TOK] = res.results[c]["out"]
    return out
```

```python
import math
from contextlib import ExitStack
import numpy as np
import concourse.bass as bass
import concourse.mybir as mybir
from concourse.bass_utils import run_bass_kernel_spmd

F32 = mybir.dt.float32
BF16 = mybir.dt.bfloat16
ALU = mybir.AluOpType
AF = mybir.ActivationFunctionType

NCORES = 8
D = 1024
TOK = 4096
TT = 512
NT = TOK // TT
NFC = 22
NRING = 9
MASK = -30000.0
EPS = 1e-6
DEBUG = False
SWDGE_DEPTH = 3
CONV_AHEAD = 6
N_DIRECT = 0


class Op:
    __slots__ = ("eng", "fn", "deps", "dma_sem", "done", "needs_inc", "idx", "phase")


class Sched:
    ENGS = ("pe", "act", "dve", "pool", "sp")

    def __init__(self):
        self.ops = []
        self.lastw = {}
        self.readers = {}

    UNION = ("qT", "sig", "y", "ysq", "sT", "convT", "PT", "attnT", "musq", "rstd", "nmr", "z", "Sb", "atok", "actT", "sg")

    def add(self, eng, fn, reads=(), writes=(), dma_sem=None):
        op = Op()
        op.eng, op.fn, op.deps, op.dma_sem, op.done, op.needs_inc = eng, fn, set(), dma_sem, None, False
        op.idx = len(self.ops)
        op.phase = getattr(self, 'phase', '')
        reads = list(reads)
        writes = list(writes)
        if any((k[0] if isinstance(k, tuple) else k) in self.UNION for k in reads + writes) and "UNI" not in reads and "UNI" not in writes:
            reads.append("UNI")
        for k in reads:
            w = self.lastw.get(k)
            if w is not None:
                op.deps.add(w)
        for k in writes:
            w = self.lastw.get(k)
            if w is not None:
                op.deps.add(w)
            for r in self.readers.get(k, ()):
                op.deps.add(r)
        for k in reads:
            self.readers.setdefault(k, []).append(op)
        for k in writes:
            self.lastw[k] = op
            self.readers[k] = []
        op.deps.discard(op)
        if dma_sem is not None and eng == "pool":
            q = self.__dict__.setdefault("_poolq", [])
            if len(q) >= SWDGE_DEPTH:
                op.deps.add(q[-SWDGE_DEPTH])
            q.append(op)
        if eng == "pe" and dma_sem is None:
            op.deps = {d for d in op.deps if not (d.eng == "pe" and d.dma_sem is None)}
        best = {}
        keep = set()
        for d in op.deps:
            if d.dma_sem is not None:
                keep.add(d)
            elif d.eng not in best or best[d.eng].idx < d.idx:
                best[d.eng] = d
        op.deps = keep | set(best.values())
        for d in op.deps:
            d.needs_inc = True
        self.ops.append(op)
        return op

    def emit(self, nc, es, block):
        sems = {e: es.enter_context(nc.semaphore("sem_" + e)) for e in self.ENGS}
        cnt = {e: 0 for e in self.ENGS}
        dcnt = {}
        for op in self.ops:
            if op.dma_sem is not None:
                nm = op.dma_sem[0]
                dcnt[nm] = dcnt.get(nm, 0) + 16
                op.done = (nm, op.dma_sem[1], dcnt[nm])
            elif op.needs_inc:
                cnt[op.eng] += 1
                op.done = ("sem_" + op.eng, sems[op.eng], cnt[op.eng])
        per = {e: [o for o in self.ops if o.eng == e] for e in self.ENGS}

        def body(ops):
            def run(e):
                waited = {}
                for op in ops:
                    need = {}
                    for d in op.deps:
                        nm, s, v = d.done
                        if waited.get(nm, 0) < v and need.get(nm, (None, 0))[1] < v:
                            need[nm] = (s, v)
                    for nm, (s, v) in need.items():
                        e.wait_ge(s, v)
                        waited[nm] = v
                    if op.fn is None:
                        continue
                    ins = op.fn(e)
                    if op.done is not None:
                        ins.then_inc(op.done[1], 16 if op.dma_sem is not None else 1)
            return run

        block.tensor(body(per["pe"]))
        block.scalar(body(per["act"]))
        block.vector(body(per["dve"]))
        block.gpsimd(body(per["pool"]))
        block.sync(body(per["sp"]))


def build_program(nt=NT, maxops=None, marks=None):
    nc = bass.Bass("TRN2", target_bir_lowering=False)
    S = Sched()
    NTOKX = TOK + 128
    x_d = nc.dram_tensor("x", [NTOKX, D], F32, kind="ExternalInput").ap()
    small_d = nc.dram_tensor("small", [128, 168], F32, kind="ExternalInput").ap()
    ident_d = nc.dram_tensor("ident", [128, 128], F32, kind="ExternalInput").ap()
    gpost_d = nc.dram_tensor("gpost", [128, 2048], F32, kind="ExternalInput").ap()
    bias_d = nc.dram_tensor("biast", [128, 2048], F32, kind="ExternalInput").ap()
    win_d = nc.dram_tensor("win", [14, 128, 1024], F32, kind="ExternalInput").ap()
    wgu_d = nc.dram_tensor("wgu", [2 * NFC, 128, 1024], F32, kind="ExternalInput").ap()
    wdn_d = nc.dram_tensor("wdn", [NFC, 128, 1024], F32, kind="ExternalInput").ap()
    wout_d = nc.dram_tensor("wout", [8, 128, 1024], F32, kind="ExternalInput").ap()
    wpw_d = nc.dram_tensor("wpw", [128, 2048], F32, kind="ExternalInput").ap()
    out_d = nc.dram_tensor("out", [TOK, D], F32, kind="ExternalOutput").ap()
    wscr_d = nc.dram_tensor("wscr", [80, 128, 1024], BF16, kind="Internal").ap()
    if DEBUG:
        dbg_d = nc.dram_tensor("dbg", [128, 8192], F32, kind="ExternalOutput").ap()

    with ExitStack() as es:
        def sb(name, shape, dt):
            return es.enter_context(nc.sbuf_tensor("sb_" + name, shape, dt))

        semc = [0]

        def newsem(name):
            semc[0] += 1
            return (name, es.enter_context(nc.semaphore(name)))

        small = sb("small", [128, 168], F32)
        ident_f = sb("ident_f", [128, 128], F32)
        ident_b = sb("ident_b", [128, 128], BF16)
        ones_m = sb("ones_m", [128, 128], BF16)
        mhalf = sb("mhalf", [128, 4], F32)
        diag = sb("diag", [128, 4, 31, 128], BF16)
        gpost = sb("gpost", [128, 2, 1024], F32)
        biast = sb("biast", [128, 4, 512], F32)
        wout = sb("wout", [128, 8, 1024], BF16)
        wpw = sb("wpw", [128, 4, 512], BF16)
        ring = [sb("ring%d" % i, [128, 1024], BF16) for i in range(NRING)]
        ring_sem = [newsem("rs%d" % i) for i in range(NRING)]
        ring_sem_sw = [newsem("rw%d" % i) for i in range(NRING)]
        xh = [sb("xh%d" % i, [128, 4, 1024], F32) for i in range(2)]
        xh_sem = [[newsem("xs%d_%d" % (i, s)) for s in range(4)] for i in range(2)]
        xs_m = sb("xs_m", [128, 4, 1024], BF16)
        xs_f = sb("xs_f", [128, 2, 1024], BF16)
        hnT_m = sb("hnT_m", [128, 8, 512], BF16)
        hnT_f = sb("hnT_f", [128, 8, 512], BF16)
        kT = [sb("kT%d" % i, [128, 512], BF16) for i in range(2)]
        Va = [sb("Va%d" % i, [128, 4, 2, 128], BF16) for i in range(2)]
        hbuf = sb("hbuf", [128, 4, 544], BF16)
        stat = sb("stat", [128, 64], F32)
        sgh = sb("sgh", [128, 32], BF16)
        UBYTES = 50 * 1024
        uni = sb("uni", [128, UBYTES // 2], BF16)
        uoff = [0]

        def carve(nbytes):
            o = uoff[0]
            uoff[0] += nbytes
            assert uoff[0] <= UBYTES
            return o // 2

        def ubf(shape):
            n = int(np.prod(shape))
            o = carve(n * 2)
            ap = uni[:, o:o + n]
            return ap

        def uf32(n):
            o = carve(n * 4)
            return uni[:, o:o + 2 * n].bitcast(F32)

        qT = ubf([4, 512])
        sig = [ubf([512]) for _ in range(2)]
        ybuf = ubf([4, 512])
        ysq = [ubf([512]) for _ in range(4)]
        sT = ubf([4, 512])
        convT = ubf([4, 512])
        PT = [ubf([4, 512]) for _ in range(2)]
        attnT = ubf([4, 512])
        musq = uf32(512)
        rstd = uf32(512)
        nmr = uf32(512)
        zt = [uf32(512) for _ in range(2)]
        Sb01 = uf32(1024)
        Sb = [Sb01[:, 0:512], Sb01[:, 512:1024]]
        attn_tok = [ubf([512]) for _ in range(2)]
        mixer_end = uoff[0]
        t1m = [(qT, [("qT", j) for j in range(4)]), (PT[0], [("PT", 0, i) for i in range(4)]),
               (PT[1], [("PT", 1, i) for i in range(4)]), (None, [("Sb", 0), ("Sb", 1)])]
        uoff[0] = 0
        actT = ubf([NFC, 512])
        sgt = [ubf([512]) for _ in range(2)]
        assert uoff[0] <= UBYTES and mixer_end <= UBYTES
        t1f = [(actT[:, i * 2048:(i + 1) * 2048], [("actT", 4 * i + k) for k in range(4)]) for i in range(4)]

        ps_all = es.enter_context(nc.psum_tensor("ps_all", [128, 4096], F32))

        def bank(i, n=1):
            return ps_all[:, i * 512:(i + n) * 512]

        bankp = [0]

        reserved = set()
        cool = {}

        def alloc1():
            for k in list(cool):
                cool[k] -= 1
                if cool[k] <= 0:
                    del cool[k]
            while (bankp[0] % 8) in reserved or (bankp[0] % 8) in cool:
                bankp[0] += 1
            b = bankp[0] % 8
            bankp[0] += 1
            return b

        def alloc2():
            while True:
                if bankp[0] % 2:
                    bankp[0] += 1
                b = bankp[0] % 8
                if b in reserved or (b + 1) in reserved:
                    bankp[0] += 2
                    continue
                bankp[0] += 2
                return b

        def PS(b, n=1):
            return [("ps", b + i) for i in range(n)]

        def mark(name):
            S.phase = name
            if marks is not None:
                marks.append((name, len(S.ops)))

        def PE(fn, r=(), w=()):
            return S.add("pe", fn, r, w)

        def ACT(fn, r=(), w=()):
            return S.add("act", fn, r, w)

        def DVE(fn, r=(), w=()):
            return S.add("dve", fn, r, w)

        def POOL(fn, r=(), w=()):
            return S.add("pool", fn, r, w)

        def ld(queue, dst, src, key):
            sem = newsem("ld_" + key)
            S.add(queue, lambda e, d=dst, s=src: e.dma_start(out=d, in_=s), (), [key], dma_sem=sem)

        ld("sp", small[:], small_d, "small")
        ld("sp", ident_f[:], ident_d, "ident_f")
        WOUT = ["wout%d" % kc for kc in range(8)]
        WPW = ["wpw0", "wpw1"]

        POOL(lambda e: e.memset(mhalf[:], -0.5), (), ["mhalf"])
        POOL(lambda e: e.memset(stat[:, 63:64], 1.0), (), [("stat", 63)])
        POOL(lambda e: e.memset(ones_m[:], 1.0 / 512.0), (), ["ones_m"])
        for i in range(2):
            POOL(lambda e, i=i: e.memset(Va[i][:, :, :, 64:128], 1.0), (), [("Va", i)])
        POOL(lambda e: e.memset(hbuf[:, :, 0:32], 0.0), (), [("hbuf", c) for c in range(4)])
        DVE(lambda e: e.tensor_copy(out=ident_b[:], in_=ident_f[:]), ["ident_f"], ["ident_b"])
        ACT(lambda e: e.activation(out=small[:, 154:162], in_=small[:, 154:162], func=AF.Exp), ["small"], ["small"])

        ring_state = {"n": 0}

        conv_sem = [newsem("cv%d" % i) for i in range(8)]
        conv_state = {"n": 0, "done": set()}

        def convert(slot):
            if slot in conv_state["done"]:
                return
            conv_state["done"].add(slot)
            n = conv_state["n"]
            conv_state["n"] += 1
            if slot < 14:
                src = win_d[slot]
            elif slot < 58:
                src = wgu_d[slot - 14]
            else:
                src = wdn_d[slot - 58]
            S.add("pool", lambda e, slot=slot, src=src: e.dma_start(out=wscr_d[slot], in_=src), (), [("scr", slot)], dma_sem=conv_sem[n % 8])

        def ring_load(slot):
            i = ring_state["n"] % NRING
            ring_state["n"] += 1
            if ring_state["n"] <= N_DIRECT:
                S.add("pool", lambda e, i=i, slot=slot: e.dma_start(out=ring[i][:], in_=win_d[slot]), (), [("ring", i)], dma_sem=ring_sem_sw[i])
                return i
            S.add("sp", lambda e, i=i, slot=slot: e.dma_start(out=ring[i][:], in_=wscr_d[slot]), [("scr", slot)], [("ring", i)], dma_sem=ring_sem[i])
            return i

        stream = []
        halo_slots = [4, 5, 6, 10, 7, 11, 8, 12, 9, 13]
        front_slots = [4, 5, 6, 10, 7, 11, 8, 12, 9, 13]
        for sl in front_slots:
            stream.append(sl)
        for t in range(nt):
            for sl in (0, 1, 2, 3):
                stream.append(sl)
            if t + 1 < nt:
                for sl in front_slots:
                    stream.append(sl)
            for fc in range(NFC):
                stream.append(14 + 2 * fc)
                stream.append(14 + 2 * fc + 1)
            for fc in range(NFC):
                stream.append(58 + fc)
        sp_ = {"issued": 0, "used": 0}

        def prefetch():
            while sp_["issued"] < len(stream) and sp_["issued"] < sp_["used"] + NRING:
                if sp_["issued"] >= N_DIRECT:
                    for la in range(sp_["issued"], min(len(stream), sp_["issued"] + CONV_AHEAD)):
                        convert(stream[la])
                ring_load(stream[sp_["issued"]])
                sp_["issued"] += 1

        def next_w():
            i = sp_["used"] % NRING
            sp_["used"] += 1
            return ring[i], ("ring", i)

        mark("setup_done")

        def load_x(par, s, row0):
            S.add("sp", lambda e: e.dma_start(out=xh[par][:, s, :], in_=x_d[row0:row0 + 128, :]), (), [("xh", par, s)],
                  dma_sem=xh_sem[par][s])

        def store_out(par, s, row0):
            S.add("sp", lambda e: e.dma_start(out=out_d[row0:row0 + 128, :], in_=xh[par][:, s, :]), [("xh", par, s)],
                  [("out", row0)], dma_sem=xh_sem[par][s])

        stat_i = [0]

        def stat_col(n=1):
            c = stat_i[0]
            stat_i[0] += 2
            if stat_i[0] >= 60:
                stat_i[0] = 0
            return c

        NOPOOL = [False]

        def rms_rstd(src_ap, src_keys, width, extra_w=(), junk_ap=None, junk_key=()):
            c = stat_col(2)
            kss = ("stat", c)
            kr = ("stat", c + 1)
            ACT(lambda e: e.activation(out=junk_ap, in_=src_ap, func=AF.Square, accum_out=stat[:, c:c + 1]),
                list(src_keys), list(junk_key) + [kss] + list(extra_w))
            DVE(lambda e: e.tensor_scalar(out=stat[:, c:c + 1], in0=stat[:, c:c + 1], scalar1=1.0 / width, scalar2=EPS,
                                          op0=ALU.mult, op1=ALU.add), [kss], [kss])
            if NOPOOL[0]:
                ACT(lambda e: e.activation(out=stat[:, c + 1:c + 2], in_=stat[:, c:c + 1], func=AF.Sqrt), [kss], [kr])
                DVE(lambda e: e.reciprocal(out=stat[:, c + 1:c + 2], in_=stat[:, c + 1:c + 2]), [kr], [kr])
            else:
                POOL(lambda e: e.tensor_tensor(out=stat[:, c + 1:c + 2], in0=stat[:, c:c + 1], in1=mhalf[:, 0:1], op=ALU.pow),
                     [kss, "mhalf"], [kr])
            return stat[:, c + 1:c + 2], kr

        def xs_slot(s, ph):
            if ph == "m":
                return xs_m[:, s, :], [("xsm", s)]
            if s < 2:
                return xs_f[:, s, :], [("xsf", s)]
            return ybuf[:, (s - 2) * 1024:(s - 1) * 1024], [("y", 2 * (s - 2)), ("y", 2 * (s - 2) + 1)]

        def prenorm_a(par, s, ph):
            xap, kx = xs_slot(s, ph)
            src = xh[par][:, s, :]
            return rms_rstd(src, [("xh", par, s)], 1024, junk_ap=xap, junk_key=kx)

        def prenorm_b(par, s, ph, rk):
            xap, kx = xs_slot(s, ph)
            src = xh[par][:, s, :]
            r_ap, kr = rk
            ACT(lambda e: e.activation(out=xap, in_=src, func=AF.Copy, scale=r_ap), [("xh", par, s), kr], kx)

        def prenorm(par, s, ph):
            prenorm_b(par, s, ph, prenorm_a(par, s, ph))

        def transposes(s, ncols, gcol0, dst_col0, ph):
            hn = hnT_m if ph == "m" else hnT_f
            xap, kx = xs_slot(s, ph)
            if ph == "f":
                b = alloc1()
                pb = bank(b).bitcast(BF16)
                for kc in range(8):
                    PE(lambda e, kc=kc: e.transpose(pb[:, kc * 128:kc * 128 + ncols], xap[:, kc * 128:(kc + 1) * 128], ident_b[:]),
                       kx + ["ident_b"], PS(b))
                for kc in range(8):
                    DVE(lambda e, kc=kc: e.tensor_scalar(out=hn[:, kc, dst_col0:dst_col0 + ncols], in0=pb[:, kc * 128:kc * 128 + ncols],
                                                         scalar1=small[:, gcol0 + kc:gcol0 + kc + 1], scalar2=None, op0=ALU.mult),
                        PS(b) + ["small"], [("hnT" + ph, kc)])
                return
            bb = [alloc1(), alloc1()]
            pbs = [bank(bb[0]).bitcast(BF16), bank(bb[1]).bitcast(BF16)]
            for par_ in range(2):
                for kc in range(par_, 8, 2):
                    PE(lambda e, kc=kc, par_=par_: e.transpose(pbs[par_][:, (kc // 2) * 128:(kc // 2) * 128 + ncols], xap[:, kc * 128:(kc + 1) * 128], ident_b[:]),
                       kx + ["ident_b"], PS(bb[par_]))
            for kc in range(0, 8, 2):
                DVE(lambda e, kc=kc: e.tensor_scalar(out=hn[:, kc, dst_col0:dst_col0 + ncols], in0=pbs[0][:, (kc // 2) * 128:(kc // 2) * 128 + ncols],
                                                     scalar1=small[:, gcol0 + kc:gcol0 + kc + 1], scalar2=None, op0=ALU.mult),
                    PS(bb[0]) + ["small"], [("hnT" + ph, kc)])
            for kc in range(1, 8, 2):
                ACT(lambda e, kc=kc: e.activation(out=hn[:, kc, dst_col0:dst_col0 + ncols], in_=pbs[1][:, (kc // 2) * 128:(kc // 2) * 128 + ncols],
                                                  func=AF.Copy, scale=small[:, gcol0 + kc:gcol0 + kc + 1]),
                    PS(bb[1]) + ["small"], [("hnT" + ph, kc)])

        def norm_transpose(par, s, ncols, gcol0, dst_col0, ph):
            prenorm(par, s, ph)
            transposes(s, ncols, gcol0, dst_col0, ph)

        def fence(tag):
            mark("fence_" + tag)
            POOL(lambda e: e.memset(stat[:, 62:63], 0.0), ["UNI"], ["UNI"])

        HN = [("hnT", kc) for kc in range(8)]

        def proj_fm(wt, wk, ncols, ph="m"):
            hn = hnT_m if ph == "m" else hnT_f
            b = alloc1()
            for kc in range(8):
                PE(lambda e, kc=kc: e.matmul(bank(b)[:, 0:ncols], lhsT=wt[:, kc * 128:(kc + 1) * 128], rhs=hn[:, kc, 0:ncols],
                                             start=(kc == 0), stop=(kc == 7)), [wk, ("hnT" + ph, kc)], PS(b))
            return b

        def w_kv(par, ncols, halo, with_halo=False):
            nsub = ncols // 128
            wt, wk = next_w()
            b = proj_fm(wt, wk, ncols)
            if with_halo:
                bh = proj_fm(wt, wk, 128, "f")
                ACT(lambda e, bh=bh: e.activation(out=kT[1 - par][:, 384:512], in_=bank(bh)[:, 0:128], func=AF.Copy), PS(bh), [("kT", 1 - par)])
            if halo:
                ACT(lambda e, b=b: e.activation(out=kT[par][:, 384:512], in_=bank(b)[:, 0:128], func=AF.Copy), PS(b), [("kT", par)])
            else:
                ACT(lambda e, b=b: e.activation(out=kT[par][:, :], in_=bank(b), func=AF.Copy), PS(b), [("kT", par)])
            prefetch()
            wt, wk = next_w()
            b = alloc1()
            for s in range(nsub):
                for kc in range(8):
                    PE(lambda e, kc=kc, s=s, b=b, wt=wt: e.matmul(bank(b)[:, s * 128:(s + 1) * 128], lhsT=hnT_m[:, kc, s * 128:(s + 1) * 128],
                                                      rhs=wt[:, kc * 128:(kc + 1) * 128], start=(kc == 0), stop=(kc == 7)),
                       [wk, ("hnTm", kc)], PS(b))
            if halo:
                ACT(lambda e, b=b: e.activation(out=Va[par][:, 3, :, 0:64], in_=bank(b)[:, 0:128].rearrange("p (g d) -> p g d", g=2), func=AF.Copy),
                    PS(b), [("Va", par)])
            else:
                ACT(lambda e, b=b: e.activation(out=Va[par][:, :, :, 0:64], in_=bank(b).rearrange("p (s g d) -> p s g d", s=4, g=2), func=AF.Copy),
                    PS(b), [("Va", par)])
            if with_halo:
                bh = alloc1()
                for kc in range(8):
                    PE(lambda e, kc=kc, bh=bh, wt=wt: e.matmul(bank(bh)[:, 0:128], lhsT=hnT_f[:, kc, 0:128],
                                                               rhs=wt[:, kc * 128:(kc + 1) * 128], start=(kc == 0), stop=(kc == 7)),
                       [wk, ("hnTf", kc)], PS(bh))
                ACT(lambda e, bh=bh: e.activation(out=Va[1 - par][:, 3, :, 0:64], in_=bank(bh)[:, 0:128].rearrange("p (g d) -> p g d", g=2), func=AF.Copy),
                    PS(bh), [("Va", 1 - par)])
            prefetch()

        def w_q():
            for j in range(4):
                wt, wk = next_w()
                b = proj_fm(wt, wk, 512)
                ACT(lambda e, j=j, b=b: e.activation(out=qT[:, j * 512:(j + 1) * 512], in_=bank(b), func=AF.Copy), PS(b) + ["UNI"], [("qT", j)])
                prefetch()

        def w_ag(ncols, halo, chunks=(0, 1, 2, 3), with_halo=False):
            for c in chunks:
                wt, wk = next_w()
                ba = proj_fm(wt, wk, ncols)
                if with_halo:
                    bah = proj_fm(wt, wk, 128, "f")
                prefetch()
                wt, wk = next_w()
                bg = proj_fm(wt, wk, ncols)
                if with_halo:
                    bgh = proj_fm(wt, wk, 128, "f")
                    ACT(lambda e, bgh=bgh: e.activation(out=sgh[:, 0:32], in_=bank(bgh)[:, 96:128], func=AF.Tanh, scale=0.5), PS(bgh), ["sgh"])
                    DVE(lambda e, bah=bah, c=c: e.scalar_tensor_tensor(out=hbuf[:, c, 0:32], in0=sgh[:, 0:32], scalar=1.0, in1=bank(bah)[:, 96:128],
                                                                     op0=ALU.add, op1=ALU.mult), PS(bah) + ["sgh"], [("hbuf", c)])
                prefetch()
                if halo:
                    ACT(lambda e, bg=bg: e.activation(out=sig[0][:, 0:32], in_=bank(bg)[:, 96:128], func=AF.Tanh, scale=0.5), PS(bg), [("sig", 0)])
                    DVE(lambda e, ba=ba, c=c: e.scalar_tensor_tensor(out=hbuf[:, c, 0:32], in0=sig[0][:, 0:32], scalar=1.0, in1=bank(ba)[:, 96:128],
                                                                    op0=ALU.add, op1=ALU.mult), PS(ba) + [("sig", 0)], [("hbuf", c)])
                else:
                    sg = sig[c % 2]
                    ACT(lambda e, bg=bg, sg=sg: e.activation(out=sg, in_=bank(bg), func=AF.Tanh, scale=0.5), PS(bg) + ["UNI"], [("sig", c % 2)])
                    DVE(lambda e, ba=ba, c=c, sg=sg: e.scalar_tensor_tensor(out=hbuf[:, c, 32:544], in0=sg, scalar=1.0, in1=bank(ba),
                                                                           op0=ALU.add, op1=ALU.mult), PS(ba) + [("sig", c % 2)], [("hbuf", c)])

        def attn_scores(t, bl, par):
            first = (t == 0 and bl == 0)
            ptb = PT[bl % 2]
            for kt in range(2):
                if kt == 1:
                    ksrc, kcol, kkey = kT[par], bl * 128, ("kT", par)
                elif bl > 0:
                    ksrc, kcol, kkey = kT[par], (bl - 1) * 128, ("kT", par)
                else:
                    ksrc, kcol, kkey = kT[1 - par], 384, ("kT", 1 - par)
                for g in range(2):
                    b = alloc1()
                    q_rhs = qT[g * 64:(g + 1) * 64, :].rearrange("p (j t) -> p j t", j=4)[:, :, bl * 128:(bl + 1) * 128]
                    PE(lambda e, b=b, g=g, ksrc=ksrc, kcol=kcol, q_rhs=q_rhs: e.matmul(
                        bank(b).rearrange("p (j t) -> p j t", j=4), lhsT=ksrc[g * 64:(g + 1) * 64, kcol:kcol + 128], rhs=q_rhs,
                        start=True, stop=True), [kkey] + [("qT", j) for j in range(4)], PS(b))
                    sbuf_ = Sb[(kt * 2 + g) % 2]
                    bt = biast[:, (2 * (1 - kt)) + g, :]
                    DVE(lambda e, b=b, sbuf_=sbuf_, bt=bt: e.scalar_tensor_tensor(out=sbuf_, in0=bank(b), scalar=0.125, in1=bt,
                                                                                  op0=ALU.mult, op1=ALU.add),
                        PS(b) + ["biast", "UNI"], [("Sb", (kt * 2 + g) % 2)])
                    pslice = ptb[:, (kt * 2 + g) * 512:(kt * 2 + g + 1) * 512]
                    pk = ("PT", bl % 2, kt * 2 + g)
                    ACT(lambda e, sbuf_=sbuf_, pslice=pslice: e.activation(out=pslice, in_=sbuf_, func=AF.Exp),
                        [("Sb", (kt * 2 + g) % 2)], [pk])
                    if first and kt == 0:
                        DVE(lambda e, pslice=pslice: e.tensor_scalar(out=pslice, in0=pslice, scalar1=small[:, 152:153], scalar2=None,
                                                                     op0=ALU.mult), [pk, "small"], [pk])

        def attn_pv(t, bl, par):
            ptb = PT[bl % 2]
            atok = attn_tok[bl % 2]
            for g in range(2):
                b = alloc1()
                for j in range(4):
                    for kt in range(2):
                        if kt == 1:
                            vsrc, vkey = Va[par][:, bl, g, 0:65], ("Va", par)
                        elif bl > 0:
                            vsrc, vkey = Va[par][:, bl - 1, g, 0:65], ("Va", par)
                        else:
                            vsrc, vkey = Va[1 - par][:, 3, g, 0:65], ("Va", 1 - par)
                        pslice = ptb[:, (kt * 2 + g) * 512 + j * 128:(kt * 2 + g) * 512 + (j + 1) * 128]
                        PE(lambda e, b=b, vsrc=vsrc, pslice=pslice, kt=kt, j=j: e.matmul(bank(b)[:, j * 65:(j + 1) * 65], lhsT=pslice, rhs=vsrc,
                                                                                         start=(kt == 0), stop=(kt == 1)),
                           [vkey, ("PT", bl % 2, kt * 2 + g)], PS(b))
                c = stat_col()
                stat_col()
                kd = ("stat", c)
                ov = bank(b)[:, 0:260].rearrange("p (j d) -> p j d", j=4)
                DVE(lambda e, ov=ov, c=c, g=g: e.tensor_tensor(out=stat[:, c:c + 4], in0=ov[:, :, 64], in1=small[:, 154 + 4 * g:158 + 4 * g], op=ALU.add),
                    PS(b) + ["small"], [kd, ("stat", c + 1), ("stat", c + 2), ("stat", c + 3)])
                DVE(lambda e, c=c: e.reciprocal(out=stat[:, c:c + 4], in_=stat[:, c:c + 4]), [kd], [kd, ("stat", c + 1), ("stat", c + 2), ("stat", c + 3)])
                for j in range(4):
                    hh = 4 * g + j
                    DVE(lambda e, ov=ov, c=c, j=j, hh=hh, atok=atok: e.tensor_scalar(out=atok[:, hh * 64:(hh + 1) * 64], in0=ov[:, j, 0:64],
                                                                                     scalar1=stat[:, c + j:c + j + 1], scalar2=None, op0=ALU.mult),
                        PS(b) + [kd], [("atok", bl % 2, g)])

        def attn_T(bl):
            atok = attn_tok[bl % 2]
            b = alloc1()
            pb = bank(b).bitcast(BF16)
            for cc in range(4):
                PE(lambda e, cc=cc, pb=pb, atok=atok: e.transpose(pb[:, cc * 128:(cc + 1) * 128], atok[:, cc * 128:(cc + 1) * 128], ident_b[:]),
                   [("atok", bl % 2, 0), ("atok", bl % 2, 1), "ident_b"], PS(b))
            dst = attnT.rearrange("p (c t) -> p c t", c=4)[:, :, bl * 128:(bl + 1) * 128]
            ACT(lambda e, pb=pb, dst=dst: e.activation(out=dst, in_=pb[:, 0:512].rearrange("p (c t) -> p c t", c=4), func=AF.Copy),
                PS(b) + ["UNI"], [("attnT", bl)])

        def conv_chunk(c):
            b = alloc1()
            for tau in range(31):
                PE(lambda e, tau=tau: e.matmul(bank(b), lhsT=diag[:, c, tau, :], rhs=hbuf[:, c, 2 + tau:2 + tau + 512],
                                               start=(tau == 0), stop=(tau == 30)), [("diag", c, tau), ("hbuf", c)], PS(b))
            ACT(lambda e: e.activation(out=ybuf[:, c * 512:(c + 1) * 512], in_=bank(b), func=AF.Identity, bias=small[:, 124 + c:125 + c]),
                PS(b) + ["small", "UNI"], [("y", c)])
            ACT(lambda e: e.activation(out=ysq[c], in_=bank(b), func=AF.Square, bias=small[:, 124 + c:125 + c]),
                PS(b) + ["small", "UNI"], [("ysq", c)])

        def mixer_tile(t, par):
            fence("m")
            w_q()
            mark("q_done")
            conv_chunk(0)
            for c in range(4):
                attn_scores(t, c, par)
                if c < 3:
                    conv_chunk(c + 1)
                if c > 0 and c < 3:
                    attn_T(c - 1)
                if c < 3:
                    attn_pv(t, c, par)
                if t + 1 < nt:
                    prenorm(1 - par, c, "m")
                    if c > 0:
                        transposes(c - 1, 128, 136, (c - 1) * 128, "m")
            mark("win_done")
            POOL(lambda e: e.tensor_copy(out=hbuf[:, :, 0:32], in_=hbuf[:, :, 512:544]), [("hbuf", c) for c in range(4)], [("hbuf", c) for c in range(4)])
            mark("attnconv_done")
            b_mu = alloc1()
            b_e2 = alloc1()
            reserved.update((b_mu, b_e2))
            for c in range(4):
                PE(lambda e, c=c: e.matmul(bank(b_mu), lhsT=ones_m[:], rhs=ybuf[:, c * 512:(c + 1) * 512], start=(c == 0), stop=(c == 3)),
                   ["ones_m", ("y", c)], PS(b_mu))
            for c in range(4):
                PE(lambda e, c=c: e.matmul(bank(b_e2), lhsT=ones_m[:], rhs=ysq[c], start=(c == 0), stop=(c == 3)),
                   ["ones_m", ("ysq", c)], PS(b_e2))
            attn_T(2)
            attn_pv(t, 3, par)
            if t + 1 < nt:
                transposes(3, 128, 136, 384, "m")
            attn_T(3)
            ACT(lambda e: e.activation(out=stat[:, 63:64], in_=stat[:, 63:64], func=AF.Sqrt), [("stat", 63)], [("stat", 63)])
            ACT(lambda e: e.activation(out=musq, in_=bank(b_mu), func=AF.Square), PS(b_mu) + ["UNI"], ["musq"])
            DVE(lambda e: e.tensor_tensor(out=rstd, in0=bank(b_e2), in1=musq, op=ALU.subtract), PS(b_e2) + ["musq"], ["rstd"])
            ACT(lambda e: e.activation(out=rstd, in_=rstd, func=AF.Sqrt, bias=small[:, 153:154]), ["rstd", "small"], ["rstd"])
            ACT(lambda e: e.activation(out=stat[:, 63:64], in_=stat[:, 63:64], func=AF.Silu), [("stat", 63)], [("stat", 63)])
            DVE(lambda e: e.reciprocal(out=rstd, in_=rstd), ["rstd"], ["rstd"])
            DVE(lambda e: e.scalar_tensor_tensor(out=nmr, in0=bank(b_mu), scalar=-1.0, in1=rstd, op0=ALU.mult, op1=ALU.mult),
                PS(b_mu) + ["rstd"], ["nmr"])
            reserved.clear()
            cool[b_mu] = 6
            cool[b_e2] = 6
            for c in range(4):
                z = zt[c % 2]
                DVE(lambda e, c=c, z=z: e.tensor_tensor(out=z, in0=ybuf[:, c * 512:(c + 1) * 512], in1=rstd, op=ALU.mult),
                    [("y", c), "rstd"], [("z", c % 2)])
                DVE(lambda e, z=z: e.tensor_tensor(out=z, in0=z, in1=nmr, op=ALU.add), [("z", c % 2), "nmr"], [("z", c % 2)])
                ACT(lambda e, c=c, z=z: e.activation(out=sT[:, c * 512:(c + 1) * 512], in_=z, func=AF.Silu,
                                                     scale=small[:, 128 + c:129 + c], bias=small[:, 132 + c:133 + c]),
                    [("z", c % 2), "small"], [("sT", c)])
            mark("ln_done")
            if t + 1 < nt:
                w_kv(1 - par, 512, False)
                w_ag(512, False, (0,))
            for co in range(4):
                b = alloc1()
                for ci in range(4):
                    PE(lambda e, b=b, co=co, ci=ci: e.matmul(bank(b), lhsT=wpw[:, ci, co * 128:(co + 1) * 128], rhs=sT[:, ci * 512:(ci + 1) * 512],
                                                             start=(ci == 0), stop=(ci == 3)), WPW + [("sT", ci)], PS(b))
                ACT(lambda e, b=b, co=co: e.activation(out=convT[:, co * 512:(co + 1) * 512], in_=bank(b), func=AF.Copy), PS(b), [("convT", co)])
            mark("pw_done")
            for s in range(4):
                b = alloc2()
                for hf in range(2):
                    for kc in range(8):
                        if kc < 4:
                            lh = attnT[:, kc * 512 + s * 128:kc * 512 + (s + 1) * 128]
                            lk = ("attnT", s)
                        else:
                            lh = convT[:, (kc - 4) * 512 + s * 128:(kc - 4) * 512 + (s + 1) * 128]
                            lk = ("convT", kc - 4)
                        PE(lambda e, b=b, hf=hf, kc=kc, lh=lh: e.matmul(bank(b + hf), lhsT=lh, rhs=wout[:, kc, hf * 512:(hf + 1) * 512],
                                                                       start=(kc == 0), stop=(kc == 7)), [lk, WOUT[kc]], PS(b + hf))
                post_norm_residual(b, par, s, 0)
            mark("wout_done")
            rks = {}
            rks[0] = prenorm_a(par, 0, "f")
            for s in range(4):
                if s + 1 < 4:
                    rks[s + 1] = prenorm_a(par, s + 1, "f")
                prenorm_b(par, s, "f", rks[s])
            nxt = t + 1 < nt
            if nxt:
                w_ag(512, False, (1,))
            transposes(0, 128, 144, 0, "f")
            transposes(1, 128, 144, 128, "f")
            if nxt:
                w_ag(512, False, (2,))
            transposes(2, 128, 144, 256, "f")
            if nxt:
                w_ag(512, False, (3,))
            transposes(3, 128, 144, 384, "f")

        def post_norm_a(b, par, s, which):
            bf, keys = (t1m if which == 0 else t1f)[s]
            if bf is None:
                tt = Sb01
                jk = Sb01.bitcast(BF16)[:, 0:1024]
            else:
                tt = bf.bitcast(F32)
                jk = bf[:, 0:1024]
            r_ap, kr = rms_rstd(bank(b, 2), PS(b, 2), 1024, extra_w=[("sq", b)], junk_ap=jk, junk_key=keys)
            DVE(lambda e: e.tensor_tensor(out=tt, in0=bank(b, 2), in1=gpost[:, which, :], op=ALU.mult),
                PS(b, 2) + [("sq", b), "gpost"], keys)
            return tt, keys, r_ap, kr

        def post_norm_b(par, s, st):
            tt, keys, r_ap, kr = st
            DVE(lambda e: e.scalar_tensor_tensor(out=xh[par][:, s, :], in0=tt, scalar=r_ap, in1=xh[par][:, s, :], op0=ALU.mult, op1=ALU.add),
                keys + [kr, ("xh", par, s)], [("xh", par, s)])

        def post_norm_residual(b, par, s, which):
            post_norm_b(par, s, post_norm_a(b, par, s, which))

        def ffn_tile(t, par):
            row0 = t * TT
            fence("f")
            for fc in range(NFC):
                wg, kg = next_w()
                bg = proj_fm(wg, kg, 512, "f")
                prefetch()
                wu, ku = next_w()
                bu = proj_fm(wu, ku, 512, "f")
                prefetch()
                sg = sgt[fc % 2]
                ACT(lambda e, bg=bg, sg=sg: e.activation(out=sg, in_=bank(bg), func=AF.Silu), PS(bg) + ["UNI"], [("sg", fc % 2)])
                DVE(lambda e, bu=bu, sg=sg, fc=fc: e.tensor_tensor(out=actT[:, fc * 512:(fc + 1) * 512], in0=bank(bu), in1=sg, op=ALU.mult),
                    PS(bu) + [("sg", fc % 2)], [("actT", fc)])
            mark("gateup_done")
            bankp[0] = 0
            for fc in range(NFC):
                wd, kd = next_w()
                for s in range(4):
                    for hf in range(2):
                        PE(lambda e, s=s, hf=hf, fc=fc, wd=wd: e.matmul(bank(2 * s + hf), lhsT=actT[:, fc * 512 + s * 128:fc * 512 + (s + 1) * 128],
                                                                       rhs=wd[:, hf * 512:(hf + 1) * 512], start=(fc == 0), stop=(fc == NFC - 1)),
                           [kd, ("actT", fc)], PS(2 * s + hf))
                prefetch()
            sts = [post_norm_a(2 * s, par, s, 1) for s in range(4)]
            for s in range(4):
                post_norm_b(par, s, sts[s])
                store_out(par, s, row0 + s * 128)

        load_x(1, 0, 0)
        for s in range(4):
            load_x(0, s, 128 + s * 128)
        prefetch()
        ld("sp", biast[:].rearrange("p a b -> p (a b)"), bias_d, "biast")
        ld("sp", gpost[:].rearrange("p a b -> p (a b)"), gpost_d, "gpost")
        mark("halo_start")
        NOPOOL[0] = True
        prenorm(1, 0, "f")
        for s_ in range(4):
            prenorm(0, s_, "m")
        NOPOOL[0] = False
        transposes(0, 128, 136, 0, "f")
        mark("halo_nt")
        for s_ in range(4):
            transposes(s_, 128, 136, s_ * 128, "m")
        w_kv(0, 512, False, with_halo=True)
        w_ag(512, False, with_halo=True)
        mark("halo_done")
        for c in range(4):
            for tau in range(31):
                eng = DVE if (tau % 2 == 0) else POOL
                eng(lambda e, c=c, tau=tau: e.tensor_scalar(out=diag[:, c, tau, :], in0=ident_f[:],
                                                            scalar1=small[:, c * 31 + tau:c * 31 + tau + 1], scalar2=0.5,
                                                            op0=ALU.mult, op1=ALU.mult),
                    ["ident_f", "small"], [("diag", c, tau)])
        for kc in range(8):
            ld("pool", wout[:, kc, :], wout_d[kc], "wout%d" % kc)
        for i in range(2):
            ld("pool", wpw[:].rearrange("p a b -> p (a b)")[:, i * 1024:(i + 1) * 1024], wpw_d[:, i * 1024:(i + 1) * 1024], "wpw%d" % i)
        for t in range(nt):
            par = t % 2
            if t + 1 < nt:
                for s in range(4):
                    load_x(1 - par, s, 128 + (t + 1) * TT + s * 128)
            mixer_tile(t, par)
            ffn_tile(t, par)
        S.add("sp", None, [("out", r) for r in range(0, nt * TT, 128)], ())

        if maxops is not None:
            S.ops = S.ops[:maxops]
        if marks is not None:
            marks.append(("__sched__", S))
        block = es.enter_context(nc.Block())
        S.emit(nc, es, block)
    return nc


def _t5_bucket(dist):
    dist = np.asarray(dist, dtype=np.int64)
    d = np.maximum(dist, 1).astype(np.float32)
    large = 16 + (np.log(d / np.float32(16)) / np.float32(math.log(128 / 16)) * np.float32(16)).astype(np.int32)
    large = np.minimum(large, 31)
    return np.where(dist < 16, dist, large)


def _prep_shared(inp):
    f = np.float32
    w_in = np.asarray(inp["w_in"], f)
    cols = []
    for j in range(4):
        cols.append(np.concatenate([np.arange(j * 64, (j + 1) * 64), np.arange((4 + j) * 64, (5 + j) * 64)]))
    cols.append(np.arange(512, 640))
    cols.append(np.arange(640, 768))
    for c in range(4):
        cols.append(np.arange(768 + c * 128, 768 + (c + 1) * 128))
    for c in range(4):
        cols.append(np.arange(1280 + c * 128, 1280 + (c + 1) * 128))

    def slotify(w, colidx):
        blk = w[:, colidx]
        return np.ascontiguousarray(blk.reshape(8, 128, 128).transpose(1, 0, 2).reshape(128, 1024))

    win = np.stack([slotify(w_in, c) for c in cols])
    w_gate = np.asarray(inp["w_gate"], f)
    w_up = np.asarray(inp["w_up"], f)
    wgu = np.empty((2 * NFC, 128, 1024), f)
    for fc in range(NFC):
        ci = np.arange(fc * 128, (fc + 1) * 128)
        wgu[2 * fc] = slotify(w_gate, ci)
        wgu[2 * fc + 1] = slotify(w_up, ci)
    wdn = np.ascontiguousarray(np.asarray(inp["w_down"], f).reshape(NFC, 128, 1024))
    w_out = np.asarray(inp["w_out"], f)
    rows = [np.arange(c * 128, (c + 1) * 128) for c in range(8)]
    wout = np.ascontiguousarray(np.stack([w_out[r, :] for r in rows]))
    w_pw = np.asarray(inp["w_conv_pw"], f)
    wpw = np.ascontiguousarray(w_pw.reshape(4, 128, 512).transpose(1, 0, 2).reshape(128, 2048))
    small = np.zeros((128, 168), f)
    dw = np.asarray(inp["conv_dw"], f)
    small[:, 0:124] = dw.reshape(31, 4, 128).transpose(2, 1, 0).reshape(128, 124)
    small[:, 124:128] = np.asarray(inp["conv_dw_b"], f).reshape(4, 128).T
    small[:, 128:132] = np.asarray(inp["conv_ln_g"], f).reshape(4, 128).T
    small[:, 132:136] = np.asarray(inp["conv_ln_b"], f).reshape(4, 128).T
    small[:, 136:144] = np.asarray(inp["mix_pre_g"], f).reshape(8, 128).T
    small[:, 144:152] = np.asarray(inp["ffn_pre_g"], f).reshape(8, 128).T
    small[:, 153] = EPS
    gpost = np.empty((128, 2048), f)
    gpost[:, 0:1024] = np.asarray(inp["mix_post_g"], f)[None, :]
    gpost[:, 1024:2048] = np.asarray(inp["ffn_post_g"], f)[None, :]
    rel = np.asarray(inp["rel_bias"], f)
    k = np.arange(128)[:, None]
    q = np.arange(128)[None, :]
    d_cur = q - k
    d_prev = q + 128 - k
    bc = _t5_bucket(np.maximum(d_cur, 0))
    bp = _t5_bucket(np.minimum(np.maximum(d_prev, 0), 255))
    biast = np.empty((128, 4, 4, 128), f)
    for g in range(2):
        for j in range(4):
            hh = 4 * g + j
            biast[:, g, j, :] = np.where(d_cur >= 0, rel[bc, hh], f(MASK))
            biast[:, 2 + g, j, :] = np.where(d_prev < 128, rel[bp, hh], f(MASK))
    biast = np.ascontiguousarray(biast.reshape(128, 2048))
    small[:, 154:162] = np.asarray(inp["attn_sinks"], f)[None, :]
    ident = np.eye(128, dtype=f)
    return dict(win=win, wgu=wgu, wdn=wdn, wout=wout, wpw=wpw, gpost=gpost, biast=biast, ident=ident), small


_NC_CACHE = {}


def kernel(**inputs):
    x = np.asarray(inputs["x"], np.float32)
    shared, small = _prep_shared(inputs)
    in_maps = []
    for c in range(NCORES):
        b, half = c // 2, c % 2
        xc = np.zeros((TOK + 128, D), np.float32)
        if half == 0:
            xc[128:] = x[b, 0:TOK]
        else:
            xc[:] = x[b, TOK - 128:2 * TOK]
        sm = small.copy()
        sm[:, 152] = 1.0 if half == 1 else 0.0
        m = dict(shared)
        m["x"] = xc
        m["small"] = sm
        in_maps.append(m)
    if "nc" not in _NC_CACHE:
        _NC_CACHE["nc"] = build_program()
    nc = _NC_CACHE["nc"]
    res = run_bass_kernel_spmd(nc, in_maps, core_ids=list(range(NCORES)))
    out = np.empty((4, 8192, D), np.float32)
    for c in range(NCORES):
        b, half = c // 2, c % 2
        out[b, half * TOK:(half + 1) * TOK] = res.results[c]["out"]
    return out
```
